# Optimizing a Trainium2 kernel written in Bass

```python
import jax, jax.numpy as jnp
from jax import lax
import numpy as np

D_MODEL = 2048
BATCH = 1
SEQ = 16384
DEPTH = 2

POOL_WINDOWS = (2, 4, 8, 16)
POOL_GROUPS = 4
POOL_CH = D_MODEL // 16
POOL_W = POOL_GROUPS * POOL_CH
GMLP_GROUPS = 4
GMLP_CH = D_MODEL // 16
GMLP_W = GMLP_GROUPS * GMLP_CH
GMLP_CHUNK = 128
HEAD_DIM = 64
NSA_HEADS = D_MODEL // 128
NSA_KV_HEADS = 4
NSA_HPG = NSA_HEADS // NSA_KV_HEADS
NSA_Q_W = NSA_HEADS * HEAD_DIM
NSA_KV_W = NSA_KV_HEADS * HEAD_DIM
CMP_BLOCK = 32
CMP_STRIDE = 16
CMP_HIDDEN = 128
SEL_BLOCK = 64
SEL_TOPN = 16
WINDOW = 512
Q_BLOCK = 128
FORCE_BONUS = 1000.0
NEG = -1e30
N_BRANCH = 3
D_FF = 4 * D_MODEL
N_MOD = 6
IN_SPLITS = (POOL_W, 2 * GMLP_W, NSA_Q_W, 6 * NSA_KV_W, 3 * NSA_HEADS, N_BRANCH * D_MODEL)
D_IN = sum(IN_SPLITS)

kernel_name = 'hybrid_pool_gmlp_nsa_block'


def rms_norm(x, g, eps=1e-6):
    xf = x.astype(jnp.float32)
    y = xf * lax.rsqrt(jnp.mean(xf * xf, axis=-1, keepdims=True) + eps)
    return (y * g.astype(jnp.float32)).astype(x.dtype)


def masked_softmax(s, mask):
    s = jnp.where(mask, s, NEG)
    m = jnp.max(s, axis=-1, keepdims=True)
    e = jnp.where(mask, jnp.exp(s - m), 0.0)
    return e / jnp.maximum(jnp.sum(e, axis=-1, keepdims=True), 1e-30)


def pool_mixer(a, w_grp, scale):
    b, s, _ = a.shape
    af = a.astype(jnp.float32).reshape(b, s, POOL_GROUPS, POOL_CH)
    cs = jnp.cumsum(af, axis=1)
    t = jnp.arange(s)
    outs = []
    for gi, w in enumerate(POOL_WINDOWS):
        c_g = cs[:, :, gi]
        lo = jnp.pad(c_g[:, :s - w], ((0, 0), (w, 0), (0, 0)))
        cnt = jnp.minimum(t + 1, w).astype(jnp.float32)[None, :, None]
        outs.append((c_g - lo) / cnt - af[:, :, gi])
    pooled = jnp.stack(outs, axis=2).astype(a.dtype)
    y = jnp.einsum('bsgc,gcd->bsgd', pooled, w_grp)
    return y.reshape(b, s, POOL_W) * scale


def gmlp_mixer(uv, ln_g, ln_b, w_s, b_s):
    b, s, _ = uv.shape
    uv = jax.nn.gelu(uv)
    u, v = jnp.split(uv, 2, axis=-1)
    vf = v.astype(jnp.float32)
    mu = jnp.mean(vf, axis=-1, keepdims=True)
    var = jnp.mean(jnp.square(vf - mu), axis=-1, keepdims=True)
    v = ((vf - mu) * lax.rsqrt(var + 1e-5) * ln_g + ln_b).astype(uv.dtype)
    vc = v.reshape(b, s // GMLP_CHUNK, GMLP_CHUNK, GMLP_GROUPS, GMLP_CH)
    causal = jnp.tril(jnp.ones((GMLP_CHUNK, GMLP_CHUNK), w_s.dtype))
    mixed = jnp.einsum('gij,bnjgc->bnigc', w_s * causal, vc) + b_s.T[None, None, :, :, None]
    return u * mixed.reshape(b, s, GMLP_W)


def compress_kv(kv, pos, w1, b1, w2, b2):
    b, s, g, d = kv.shape
    n_cmp = (s - CMP_BLOCK) // CMP_STRIDE + 1
    idx = CMP_STRIDE * jnp.arange(n_cmp)[:, None] + jnp.arange(CMP_BLOCK)[None, :]
    blk = kv[:, idx] + pos[None, None, :, None, :]
    blk = jnp.swapaxes(blk, 2, 3).reshape(b, n_cmp, g, CMP_BLOCK * d)
    hid = jax.nn.gelu(blk @ w1 + b1)
    return hid @ w2 + b2


def nsa_mixer(q, kc, vc, ks, vs, kw, vw, gate_logits, cmp_pos, cmp_w1, cmp_b1, cmp_w2, cmp_b2):
    b, s = q.shape[:2]
    G, P = NSA_KV_HEADS, NSA_HPG
    q = q.reshape(b, s, G, P, HEAD_DIM)
    gates = jax.nn.sigmoid(gate_logits.astype(jnp.float32)).reshape(b, s, G, P, 3)
    k_cmp = compress_kv(kc.reshape(b, s, G, HEAD_DIM), cmp_pos[0], cmp_w1[0], cmp_b1[0], cmp_w2[0], cmp_b2[0])
    v_cmp = compress_kv(vc.reshape(b, s, G, HEAD_DIM), cmp_pos[1], cmp_w1[1], cmp_b1[1], cmp_w2[1], cmp_b2[1])
    n_cmp = k_cmp.shape[1]
    cmp_start = jnp.arange(n_cmp) * CMP_STRIDE
    cmp_end = cmp_start + CMP_BLOCK - 1
    n_sel = s // SEL_BLOCK
    n_top = min(SEL_TOPN, n_sel)
    sel_start = jnp.arange(n_sel) * SEL_BLOCK
    overlap = ((cmp_start[:, None] < sel_start[None, :] + SEL_BLOCK)
               & (cmp_end[:, None] >= sel_start[None, :])).astype(jnp.float32)
    ks_blk = ks.reshape(b, n_sel, SEL_BLOCK, G, HEAD_DIM).transpose(0, 3, 1, 2, 4)
    vs_blk = vs.reshape(b, n_sel, SEL_BLOCK, G, HEAD_DIM).transpose(0, 3, 1, 2, 4)
    kw_pad = jnp.pad(kw.reshape(b, s, G, HEAD_DIM), ((0, 0), (WINDOW, 0), (0, 0), (0, 0)))
    vw_pad = jnp.pad(vw.reshape(b, s, G, HEAD_DIM), ((0, 0), (WINDOW, 0), (0, 0), (0, 0)))
    scale = HEAD_DIM ** -0.5
    bi = jnp.arange(b)[:, None, None, None]
    gi = jnp.arange(G)[None, :, None, None]
    jj = jnp.arange(n_sel)

    def block(qb):
        t0 = qb * Q_BLOCK
        tpos = t0 + jnp.arange(Q_BLOCK)
        qblk = lax.dynamic_slice_in_dim(q, t0, Q_BLOCK, axis=1)
        gblk = lax.dynamic_slice_in_dim(gates, t0, Q_BLOCK, axis=1)
        s_c = jnp.einsum('bqgpd,bngd->bgpqn', qblk, k_cmp).astype(jnp.float32) * scale
        p_c = masked_softmax(s_c, cmp_end[None, :] <= tpos[:, None])
        o_c = jnp.einsum('bgpqn,bngd->bqgpd', p_c.astype(v_cmp.dtype), v_cmp)
        imp = jnp.einsum('bgqn,nj->bgqj', p_c.sum(axis=2), overlap)
        cur = tpos // SEL_BLOCK
        forced = (jj[None, :] == 0) | (jj[None, :] == cur[:, None]) | (jj[None, :] == cur[:, None] - 1)
        valid = jj[None, :] <= cur[:, None]
        score = jnp.where(valid, imp + FORCE_BONUS * forced.astype(jnp.float32), -1.0)
        _, idx = lax.top_k(score, n_top)
        kg = ks_blk[bi, gi, idx]
        vg = vs_blk[bi, gi, idx]
        keypos = idx[..., None] * SEL_BLOCK + jnp.arange(SEL_BLOCK)
        mask_s = (keypos <= tpos[None, None, :, None, None]).reshape(b, G, 1, Q_BLOCK, n_top * SEL_BLOCK)
        s_s = jnp.einsum('bqgpd,bgqnkd->bgpqnk', qblk, kg).astype(jnp.float32) * scale
        p_s = masked_softmax(s_s.reshape(b, G, P, Q_BLOCK, n_top * SEL_BLOCK), mask_s)
        o_s = jnp.einsum('bgpqm,bgqmd->bqgpd', p_s.astype(vg.dtype),
                         vg.reshape(b, G, Q_BLOCK, n_top * SEL_BLOCK, HEAD_DIM))
        kwb = lax.dynamic_slice_in_dim(kw_pad, t0, Q_BLOCK + WINDOW, axis=1)
        vwb = lax.dynamic_slice_in_dim(vw_pad, t0, Q_BLOCK + WINDOW, axis=1)
        kpos = t0 - WINDOW + jnp.arange(Q_BLOCK + WINDOW)
        rel = tpos[:, None] - kpos[None, :]
        mask_w = (rel >= 0) & (rel < WINDOW) & (kpos[None, :] >= 0)
        s_w = jnp.einsum('bqgpd,bkgd->bgpqk', qblk, kwb).astype(jnp.float32) * scale
        p_w = masked_softmax(s_w, mask_w)
        o_w = jnp.einsum('bgpqk,bkgd->bqgpd', p_w.astype(vwb.dtype), vwb)
        o = gblk[..., 0:1] * o_c + gblk[..., 1:2] * o_s + gblk[..., 2:3] * o_w
        return o.astype(q.dtype)

    out = lax.map(block, jnp.arange(s // Q_BLOCK))
    return jnp.moveaxis(out, 0, 1).reshape(b, s, NSA_Q_W)


def token_mixer(h, w_in, pool_w, pool_scale, gmlp_ln_g, gmlp_ln_b, gmlp_ws, gmlp_bs,
                cmp_pos, cmp_w1, cmp_b1, cmp_w2, cmp_b2, w_br_pool, w_br_gmlp, w_br_nsa, w_out):
    b, s, d = h.shape
    proj = h @ w_in
    cuts = list(np.cumsum(IN_SPLITS)[:-1])
    a_pool, a_gmlp, a_q, a_kv, a_ngate, a_bgate = jnp.split(proj, cuts, axis=-1)
    kc, vc, ks, vs, kw, vw = jnp.split(a_kv, 6, axis=-1)
    y_a = pool_mixer(a_pool, pool_w, pool_scale)
    y_b = gmlp_mixer(a_gmlp, gmlp_ln_g, gmlp_ln_b, gmlp_ws, gmlp_bs)
    y_c = nsa_mixer(a_q, kc, vc, ks, vs, kw, vw, a_ngate, cmp_pos, cmp_w1, cmp_b1, cmp_w2, cmp_b2)
    g = jax.nn.sigmoid(a_bgate.reshape(b, s, N_BRANCH, d))
    merged = (g[:, :, 0] * (y_a @ w_br_pool)
              + g[:, :, 1] * (y_b @ w_br_gmlp)
              + g[:, :, 2] * (y_c @ w_br_nsa))
    return merged @ w_out


def setup_inputs(seed: int = 0) -> dict:
    key = jax.random.key(seed)
    ks = jax.random.split(key, 23)

    def nrm(k, shape, sc):
        return jax.random.normal(k, shape, jnp.float32) * sc

    L = DEPTH
    return {
        'x': nrm(ks[0], (BATCH, SEQ, D_MODEL), 1.0),
        'c': nrm(ks[1], (BATCH, D_MODEL), 1.0),
        'norm_g': 1.0 + nrm(ks[2], (L, 4, D_MODEL), 0.05),
        'w_ada': nrm(ks[3], (L, D_MODEL, N_MOD * D_MODEL), 0.5 * D_MODEL ** -0.5),
        'b_ada': nrm(ks[4], (L, N_MOD * D_MODEL), 0.02),
        'w_in': nrm(ks[5], (L, D_MODEL, D_IN), D_MODEL ** -0.5),
        'pool_w': nrm(ks[6], (L, POOL_GROUPS, POOL_CH, POOL_CH), POOL_CH ** -0.5),
        'pool_scale': 1.0 + nrm(ks[7], (L, POOL_W), 0.1),
        'gmlp_ln_g': 1.0 + nrm(ks[8], (L, GMLP_W), 0.05),
        'gmlp_ln_b': nrm(ks[9], (L, GMLP_W), 0.02),
        'gmlp_ws': nrm(ks[10], (L, GMLP_GROUPS, GMLP_CHUNK, GMLP_CHUNK), 0.5 * GMLP_CHUNK ** -0.5),
        'gmlp_bs': 1.0 + nrm(ks[11], (L, GMLP_GROUPS, GMLP_CHUNK), 0.1),
        'cmp_pos': nrm(ks[12], (L, 2, CMP_BLOCK, HEAD_DIM), 0.1),
        'cmp_w1': nrm(ks[13], (L, 2, CMP_BLOCK * HEAD_DIM, CMP_HIDDEN), (CMP_BLOCK * HEAD_DIM) ** -0.5),
        'cmp_b1': nrm(ks[14], (L, 2, CMP_HIDDEN), 0.02),
        'cmp_w2': nrm(ks[15], (L, 2, CMP_HIDDEN, HEAD_DIM), CMP_HIDDEN ** -0.5),
        'cmp_b2': nrm(ks[16], (L, 2, HEAD_DIM), 0.02),
        'w_br_pool': nrm(ks[17], (L, POOL_W, D_MODEL), POOL_W ** -0.5),
        'w_br_gmlp': nrm(ks[18], (L, GMLP_W, D_MODEL), GMLP_W ** -0.5),
        'w_br_nsa': nrm(ks[19], (L, NSA_Q_W, D_MODEL), NSA_Q_W ** -0.5),
        'w_out': nrm(ks[20], (L, D_MODEL, D_MODEL), D_MODEL ** -0.5),
        'w_ff1': nrm(ks[21], (L, D_MODEL, D_FF), D_MODEL ** -0.5),
        'w_ff2': nrm(ks[22], (L, D_FF, D_MODEL), D_FF ** -0.5),
    }


def reference(x, c, norm_g, w_ada, b_ada, w_in, pool_w, pool_scale, gmlp_ln_g, gmlp_ln_b,
              gmlp_ws, gmlp_bs, cmp_pos, cmp_w1, cmp_b1, cmp_w2, cmp_b2,
              w_br_pool, w_br_gmlp, w_br_nsa, w_out, w_ff1, w_ff2):
    b, s, d = x.shape
    c_act = jax.nn.silu(c)
    for l in range(DEPTH):
        mod = (c_act @ w_ada[l] + b_ada[l]).reshape(b, N_MOD, 1, d)
        shift1, scale1, gate1, shift2, scale2, gate2 = [mod[:, i] for i in range(N_MOD)]
        h = rms_norm(x, norm_g[l, 0]) * (1.0 + scale1) + shift1
        y = token_mixer(h, w_in[l], pool_w[l], pool_scale[l], gmlp_ln_g[l], gmlp_ln_b[l],
                        gmlp_ws[l], gmlp_bs[l], cmp_pos[l], cmp_w1[l], cmp_b1[l], cmp_w2[l],
                        cmp_b2[l], w_br_pool[l], w_br_gmlp[l], w_br_nsa[l], w_out[l])
        x = x + gate1 * rms_norm(y, norm_g[l, 1])
        h = rms_norm(x, norm_g[l, 2]) * (1.0 + scale2) + shift2
        f = jnp.square(jax.nn.relu(h @ w_ff1[l])) @ w_ff2[l]
        x = x + gate2 * rms_norm(f, norm_g[l, 3])
    return x
```

```python
from contextlib import ExitStack
import numpy as np
import concourse.bass as bass
import concourse.mybir as mybir

F32 = mybir.dt.float32
BF16 = mybir.dt.bfloat16
AF = mybir.ActivationFunctionType
ALU = mybir.AluOpType
AX = mybir.AxisListType


class Buf:
    def __init__(self, name, ap=None):
        self.name = name
        self.ap = ap
        self.w = {}
        self.r = {}
        self.sem = None
        self.cnt = 0
        self.aliases = []
        self.wl = []

    def __getitem__(self, idx):
        return self.ap[idx]


class Prog:
    ENG = ("pe", "act", "dve", "pool", "sp")

    def __init__(self, nc, es):
        self.nc = nc
        self.es = es
        self.streams = {e: [] for e in self.ENG}
        self.cnt = {e: 0 for e in self.ENG}
        self.known = {e: {} for e in self.ENG}
        self.sems = {}
        for e in ("pe", "act", "dve", "pool"):
            self.sems[e] = es.enter_context(nc.semaphore("sem_" + e))
        self.nsem = 4
        self.dma_bufs = []
        self.dcnt = {}
        self.n_inst = 0

    def sbuf(self, name, shape, dt):
        t = self.es.enter_context(self.nc.sbuf_tensor(name, list(shape), dt))
        return Buf(name, t[tuple(slice(None) for _ in shape)])

    def psum(self, name, shape, dt=F32):
        t = self.es.enter_context(self.nc.psum_tensor(name, list(shape), dt))
        return Buf(name, t[tuple(slice(None) for _ in shape)])

    def dram(self, name, shape, dt, kind="Internal"):
        t = self.nc.dram_tensor(name, list(shape), dt, kind=kind)
        return Buf(name, t.ap())

    def _dsem(self, buf, q):
        cls = "sw" if q == "pool" else "hw"
        key = (id(buf), cls)
        if key not in self.sems:
            self.sems[key] = self.es.enter_context(self.nc.semaphore("dsem_%d_%s_%s" % (self.nsem, buf.name, cls)))
            self.nsem += 1
            self.dcnt[key] = 0
            if buf not in self.dma_bufs:
                self.dma_bufs.append(buf)
        return key

    def _deps(self, eng, reads, writes, dma_dst=None):
        need = {}

        def add(k, v, kind):
            if k == eng:
                if eng == "pe" or kind == "waw":
                    return
            if need.get(k, 0) < v:
                need[k] = v

        for b in reads:
            for k, v in b.w.items():
                add(k, v, "raw")
        for b in writes:
            for k, v in b.w.items():
                if dma_dst is b and not isinstance(k, str):
                    continue
                add(k, v, "waw")
            for k, v in b.r.items():
                add(k, v, "war")
        waits = []
        kn = self.known[eng]
        for k, v in need.items():
            if kn.get(k, 0) < v:
                kn[k] = v
                waits.append((k, v))
        return waits

    @staticmethod
    def _expand(bufs):
        out = []
        for b in bufs:
            if b not in out:
                out.append(b)
            for a in b.aliases:
                if a not in out:
                    out.append(a)
        return out

    def alias(self, big, smalls):
        for s_ in smalls:
            big.aliases.append(s_)
            s_.aliases.append(big)

    def view(self, name, ap):
        return Buf(name, ap)

    @staticmethod
    def _box(ap):
        try:
            t = ap.tensor
            shp = list(t.shape)
            per = 1
            for d in shp[1:]:
                per *= int(d)
            off = int(ap.offset)
            dims = [(int(a), int(b)) for a, b in ap.ap]
            p0 = off // per
            f0 = off % per
            pst, pc = dims[0]
            p1 = p0 + (pc - 1) * (pst // per if per else 0)
            f1 = f0 + sum(abs(st) * (c - 1) for st, c in dims[1:])
            return (t.name, p0, p1, f0, f1)
        except Exception:
            return None

    @staticmethod
    def _overlap(a, b):
        if a is None or b is None:
            return True
        if a[0] != b[0]:
            return False
        return not (a[2] < b[1] or b[2] < a[1] or a[4] < b[3] or b[4] < a[3])

    def op(self, eng, fn, reads=(), writes=(), wap=None):
        reads = self._expand(reads)
        writes = self._expand(writes)
        waits = self._deps(eng, reads, writes)
        box = self._box(wap) if wap is not None else None
        if eng != "pe":
            kn = self.known[eng]
            need = 0
            for b in writes:
                for (e2, c2, bx) in b.wl:
                    if e2 == eng and c2 > kn.get(eng, 0) and self._overlap(box, bx):
                        need = max(need, c2)
            if need > kn.get(eng, 0):
                kn[eng] = need
                waits.append((eng, need))
        self.cnt[eng] += 1
        ev = (eng, self.cnt[eng])
        for b in writes:
            b.wl.append((eng, ev[1], box))
            if len(b.wl) > 12:
                b.wl = b.wl[-12:]
        self.streams[eng].append((waits, fn, ev, 1))
        for b in reads:
            if b.r.get(eng, 0) < ev[1]:
                b.r[eng] = ev[1]
        for b in writes:
            b.w = {eng: ev[1]}
            b.r = {}
        self.n_inst += 1

    def dma(self, q, out_ap, in_ap, dst, src, own=None, **kw):
        if own is None:
            own = dst
        key = self._dsem(own, q)
        srcs = self._expand([src])
        dsts = self._expand([dst])
        waits = self._deps(q, srcs, dsts, dma_dst=dst)
        self.dcnt[key] += 16
        ev = (key, self.dcnt[key])
        self.streams[q].append((waits, lambda e: e.dma_start(out=out_ap, in_=in_ap, **kw), ev, 16))
        for s_ in srcs:
            if s_.r.get(ev[0], 0) < ev[1]:
                s_.r[ev[0]] = ev[1]
        for d_ in dsts:
            if any(isinstance(k, str) for k in d_.w):
                d_.w = {}
            d_.w[ev[0]] = ev[1]
            d_.r = {}
        self.n_inst += 1

    def finish(self, out_bufs):
        nc = self.nc
        fw = {}
        for b in list(out_bufs) + self.dma_bufs:
            for k, v in b.w.items():
                if not isinstance(k, str) and fw.get(k, 0) < v:
                    fw[k] = v
        for key, v in self.dcnt.items():
            if fw.get(key, 0) < v:
                fw[key] = v
        final_waits = list(fw.items())
        engmap = {"pe": "tensor", "act": "scalar", "dve": "vector", "pool": "gpsimd", "sp": "sync"}
        with nc.Block() as block:
            for e in self.ENG:
                stream = self.streams[e]
                extra = final_waits if e == "sp" else []

                def body(eng, stream=stream, extra=extra):
                    for waits, fn, ev, inc in stream:
                        for k, v in waits:
                            eng.wait_ge(self.sems[k], v)
                        inst = fn(eng)
                        inst.then_inc(self.sems[ev[0]], inc)
                    for k, v in extra:
                        eng.wait_ge(self.sems[k], v)

                getattr(block, engmap[e])(body)


T = 2048
D = 2048
EPS = 1e-6


def rms_to_hT(P, x_tile_src, t, xt_bufs, junk, ssq_b, eps_b, ident, GT, ST, ps, hT, hcol0, xdram):
    nc = P.nc
    xt = xt_bufs[t % len(xt_bufs)]
    P.dma("sp", xt.ap[:, :], x_tile_src, xt, xdram)
    ssq = ssq_b[t % len(ssq_b)]
    P.op("act", lambda e: e.activation(out=junk.ap[:, :], in_=xt.ap[:, :], func=AF.Square, accum_out=ssq.ap[:, 0:1]),
         reads=[xt], writes=[junk, ssq])
    P.op("act", lambda e: e.activation(out=ssq.ap[:, 1:2], in_=ssq.ap[:, 0:1], func=AF.Sqrt, scale=1.0 / D, bias=eps_b.ap[:, 0:1]),
         reads=[ssq, eps_b], writes=[ssq])
    P.op("dve", lambda e: e.reciprocal(out=ssq.ap[:, 2:3], in_=ssq.ap[:, 1:2]), reads=[ssq], writes=[ssq])
    P.op("dve", lambda e: e.tensor_scalar(out=xt.ap[:, :], in0=xt.ap[:, :], scalar1=ssq.ap[:, 2:3], scalar2=None, op0=ALU.mult),
         reads=[xt, ssq], writes=[xt])
    for b in range(4):
        bank = ps[(t % 2) * 4 + b]
        for i in range(4):
            kc = b * 4 + i
            P.op("pe", lambda e, kc=kc, i=i, bank=bank: e.transpose(out=bank.ap[:, i * 128:(i + 1) * 128], in_=xt.ap[:, kc * 128:(kc + 1) * 128], identity=ident.ap[:, :]),
                 reads=[xt, ident], writes=[bank])
        for i in range(4):
            kc = b * 4 + i
            P.op("act", lambda e, kc=kc, i=i, bank=bank: e.activation(out=hT.ap[:, kc, hcol0:hcol0 + 128], in_=bank.ap[:, i * 128:(i + 1) * 128],
                                                                 func=AF.Identity, scale=GT.ap[:, kc:kc + 1], bias=ST.ap[:, kc:kc + 1]),
                 reads=[bank, GT, ST], writes=[hT])


def build_A():
    nc = bass.Bass("TRN2", target_bir_lowering=False)
    with ExitStack() as es:
        P = Prog(nc, es)
        x = P.dram("x", [T, D], F32, "ExternalInput")
        modT = P.dram("modT", [128, 6, 16], F32, "ExternalInput")
        normgT = P.dram("normgT", [128, 4, 16], F32, "ExternalInput")
        wA = P.dram("wA", [D, 2048], F32, "ExternalInput")
        identd = P.dram("ident", [128, 128], F32, "ExternalInput")
        hT_o = P.dram("hT_o", [16, 128, T], BF16, "ExternalOutput")
        apT_o = P.dram("apT_o", [4, 128, T], F32, "ExternalOutput")
        KT_o = P.dram("KT_o", [4, 4, 64, T], BF16, "ExternalOutput")
        V_o = P.dram("V_o", [2, T, 256], BF16, "ExternalOutput")
        wsrc = P.dram("wsrc", [7, 128, 8192], F32, "ExternalInput")
        wdst = P.dram("wdst", [7, 128, 8192], BF16, "ExternalOutput")

        ident = P.sbuf("ident_sb", [128, 128], F32)
        modT_sb = P.sbuf("modT_sb", [128, 6, 16], F32)
        normgT_sb = P.sbuf("normgT_sb", [128, 4, 16], F32)
        GT = P.sbuf("GT", [128, 16], F32)
        eps_b = P.sbuf("eps", [128, 1], F32)
        wsb = P.sbuf("wsb", [128, 16, 2048], BF16)
        hT = P.sbuf("hT", [128, 16, T], BF16)
        xt_bufs = [P.sbuf("xt%d" % i, [128, D], F32) for i in range(2)]
        junk = P.sbuf("junk", [128, D], BF16)
        ssq_b = [P.sbuf("ssq%d" % i, [128, 4], F32) for i in range(2)]
        stg_f = [P.sbuf("stgf%d" % i, [128, 512], F32) for i in range(2)]
        stg_b = [P.sbuf("stgb%d" % i, [128, 512], BF16) for i in range(3)]
        ps = [P.psum("ps%d" % i, [128, 512]) for i in range(8)]

        P.dma("sp", ident.ap[:, :], identd.ap[:, :], ident, identd)
        P.dma("sp", modT_sb.ap[:, :, :], modT.ap[:, :, :], modT_sb, modT)
        P.dma("sp", normgT_sb.ap[:, :, :], normgT.ap[:, :, :], normgT_sb, normgT)
        P.op("dve", lambda e: e.memset(eps_b.ap[:, :], EPS), writes=[eps_b])
        P.op("dve", lambda e: e.scalar_tensor_tensor(out=GT.ap[:, :], in0=modT_sb.ap[:, 1, :], scalar=1.0, in1=normgT_sb.ap[:, 0, :], op0=ALU.add, op1=ALU.mult),
             reads=[modT_sb, normgT_sb], writes=[GT])
        for kc in range(16):
            P.dma("pool", wsb.ap[:, kc, :], wA.ap[kc * 128:(kc + 1) * 128, :], wsb, wA)

        ST = P.sbuf("ST", [128, 16], F32)
        P.op("dve", lambda e: e.tensor_copy(out=ST.ap[:, :], in_=modT_sb.ap[:, 0, :]), reads=[modT_sb], writes=[ST])
        for t in range(16):
            rms_to_hT(P, x.ap[t * 128:(t + 1) * 128, :], t, xt_bufs, junk, ssq_b, eps_b, ident, GT, ST, ps, hT, t * 128, x)
        for kc in range(16):
            P.dma("pool", hT_o.ap[kc, :, :], hT.ap[:, kc, :], hT_o, hT, own=hT_o)

        bank_i = [0]

        def next_bank():
            b = ps[bank_i[0] % 8]
            bank_i[0] += 1
            return b
        si = [0, 0]
        for m in range(4):
            for n in range(4):
                bank = next_bank()
                for kc in range(16):
                    P.op("pe", lambda e, kc=kc, m=m, n=n, bank=bank: e.matmul(bank.ap[:, :], lhsT=wsb.ap[:, kc, m * 128:(m + 1) * 128], rhs=hT.ap[:, kc, n * 512:(n + 1) * 512],
                                                                           start=(kc == 0), stop=(kc == 15)),
                         reads=[wsb, hT], writes=[bank])
                st = stg_f[si[0] % 2]
                si[0] += 1
                P.op("act", lambda e, bank=bank, st=st: e.copy(out=st.ap[:, :], in_=bank.ap[:, :]), reads=[bank], writes=[st])
                P.dma("pool", apT_o.ap[m, :, n * 512:(n + 1) * 512], st.ap[:, :], apT_o, st, own=st)
        for kind, base in enumerate([512, 768, 1024, 1536]):
            for mp in range(2):
                for n in range(4):
                    bank = next_bank()
                    for kc in range(16):
                        P.op("pe", lambda e, kc=kc, n=n, bank=bank, c0=base + mp * 128: e.matmul(bank.ap[:, :], lhsT=wsb.ap[:, kc, c0:c0 + 128], rhs=hT.ap[:, kc, n * 512:(n + 1) * 512],
                                                                                       start=(kc == 0), stop=(kc == 15)),
                             reads=[wsb, hT], writes=[bank])
                    st = stg_b[si[1] % 3]
                    si[1] += 1
                    P.op("dve", lambda e, bank=bank, st=st: e.tensor_copy(out=st.ap[:, :], in_=bank.ap[:, :]), reads=[bank], writes=[st])
                    for gg in range(2):
                        P.dma("pool", KT_o.ap[kind, mp * 2 + gg, :, n * 512:(n + 1) * 512], st.ap[gg * 64:(gg + 1) * 64, :], KT_o, st, own=st)
        for kind, base in enumerate([1280, 1792]):
            for t in range(16):
                bank = next_bank()
                for kc in range(16):
                    P.op("pe", lambda e, kc=kc, t=t, bank=bank, base=base: e.matmul(bank.ap[:, 0:256], lhsT=hT.ap[:, kc, t * 128:(t + 1) * 128], rhs=wsb.ap[:, kc, base:base + 256],
                                                                                 start=(kc == 0), stop=(kc == 15)),
                         reads=[wsb, hT], writes=[bank])
                st = stg_b[si[1] % 3]
                si[1] += 1
                P.op("act", lambda e, bank=bank, st=st: e.copy(out=st.ap[:, 0:256], in_=bank.ap[:, 0:256]), reads=[bank], writes=[st])
                P.dma("pool", V_o.ap[kind, t * 128:(t + 1) * 128, :], st.ap[:, 0:256], V_o, st, own=st)
        for b_ in range(7):
            for q_ in range(4):
                P.dma("pool", wdst.ap[b_, :, q_ * 2048:(q_ + 1) * 2048], wsrc.ap[b_, :, q_ * 2048:(q_ + 1) * 2048], wdst, wsrc)
        P.finish([hT_o, apT_o, KT_o, V_o, wdst])
        print("phase A instructions:", P.n_inst)
    return nc


T = 2048
D = 2048
NT = 256
NGRP = 8
TP = 16384 + 128
EPS = 1e-6
GELU_C = 1.5957691216057308
WU, WV, WQ, WM, WO, WF1, WF2 = 0, 1, 2, 4, 20, 24, 40
NBLK = 56


def build_B(ngrp=NGRP, do_attn=True):
    nc = bass.Bass("TRN2", target_bir_lowering=False)
    with ExitStack() as es:
        P = Prog(nc, es)
        I = lambda name, shape, dt=F32: P.dram(name, shape, dt, "ExternalInput")
        x = I("x", [T, D])
        hTd = I("hT", [16, 128, T], BF16)
        apTd = I("apT", [4, 128, T])
        halod = I("halo", [4, 128, 16, 16])
        modTd = I("modT", [128, 6, 16])
        modrow = I("modrow", [6, D])
        normg = I("normg", [4, D])
        normgTd = I("normgT", [128, 4, 16])
        KTall = I("KTall", [3, 4, 64, TP], BF16)
        Vall = I("Vall", [16384, 4, 66], BF16)
        kwTw = I("kwTw", [4, 64, 16, 640], BF16)
        vww = I("vww", [16, 640, 4, 66], BF16)
        WT = I("WT", [NBLK, 128, 8192], BF16)
        wngd = I("wng", [128, 16, 48])
        poolwd = I("poolw", [128, 4, 128])
        pscd = I("pscT", [128, 4])
        lngd = I("lng", [1, 512])
        lnbd = I("lnb", [1, 512])
        wsd = I("ws", [128, 4, 128])
        bsd = I("bs", [1, 512])
        trild = I("tril", [128, 128])
        posTd = I("posT", [64, 2, 32])
        w1d = I("w1t", [64, 2, 32, 128])
        b1d = I("b1T", [128, 2])
        w2d = I("w2", [128, 2, 64])
        b2kd = I("b2kT", [64, 1])
        b2vd = I("b2v", [1, 64])
        identd = I("ident", [128, 128])
        cmaskd = I("cmask", [128, 8, 128], BF16)
        cmpmd = I("cmpm", [128, 2, 128], BF16)
        wmaskd = I("wmask", [128, 2, 5, 128], BF16)
        bonusd = I("bonus", [16, 128, 256], BF16)
        ovbd = I("ovb", [128, 480], BF16)
        invcd = I("invc", [128, 4, 128])
        xo = P.dram("xo", [T, D], F32, "ExternalOutput")

        S = lambda name, shape, dt=F32: P.sbuf(name, shape, dt)
        ident = S("ident_sb", [128, 128])
        identb = S("identb", [128, 128], BF16)
        modT = S("modT_sb", [128, 6, 16])
        normgT = S("normgT_sb", [128, 4, 16])
        G2T = S("G2T", [128, 16])
        S2T = S("S2T", [128, 16])
        eps_b = S("eps", [128, 1])
        eps5 = S("eps5", [128, 1])
        ggrow = [S("ggrow%d" % i, [1, D]) for i in range(2)]
        ggd = P.dram("ggd", [2, D], F32)
        wng = S("wng_sb", [128, 16, 48], BF16)
        poolw = S("poolw_sb", [128, 4, 128], BF16)
        psc = S("psc", [128, 4])
        lng = S("lng_sb", [128, 512])
        lnb = S("lnb_sb", [128, 512])
        bsb = S("bsb", [128, 512])
        WmT = S("WmT", [128, 4, 128], BF16)
        posT = S("posT_sb", [64, 2, 32], BF16)
        b1 = S("b1_sb", [128, 2])
        c1 = S("c1", [128, 2])
        w2 = S("w2_sb", [128, 2, 64], BF16)
        b2k = S("b2k", [64, 1])
        b2v = S("b2v_sb", [128, 64])
        cmask = S("cmask_sb", [128, 8, 128], BF16)
        cmpm = S("cmpm_sb", [128, 2, 128], BF16)
        wmask = S("wmask_sb", [128, 2, 5, 128], BF16)
        ovb = S("ovb_sb", [128, 480], BF16)
        invc = S("invc_sb", [128, 4, 128])
        kcmpT = S("kcmpT", [64, 4, 1024], BF16)
        vcmp = S("vcmp", [128, 4, 8, 66], BF16)
        arena = S("arena", [128, 16512], BF16)
        f1T = P.view("f1T", arena.ap[:, 0:16384].rearrange("p (m n) -> p m n", n=NT))
        qT = P.view("qT", arena.ap[0:64, 0:4096].rearrange("p (h n) -> p h n", n=NT))
        uT = P.view("uT", arena.ap[:, 4096:5120].rearrange("p (m n) -> p m n", n=NT))
        ybT = P.view("ybT", arena.ap[:, 5120:6144].rearrange("p (m n) -> p m n", n=NT))
        yaT = P.view("yaT", arena.ap[:, 6144:7168].rearrange("p (m n) -> p m n", n=NT))
        ycT = P.view("ycT", arena.ap[:, 7168:9216].rearrange("p (m n) -> p m n", n=NT))
        kcb = [P.view("kcb0", arena.ap[0:64, 0:8224]), P.view("kcb1", arena.ap[0:64, 8224:16448])]
        smalls = [qT, uT, ybT, yaT, ycT]
        P.alias(f1T, smalls)
        for kb in kcb:
            P.alias(kb, smalls + [f1T])
        yfT = S("yfT", [128, 16, NT])
        hT = S("hT_sb", [128, 16, NT], BF16)
        mergedT = P.view("mergedT", arena.ap[:, 0:4096].rearrange("p (m n) -> p m n", n=NT))
        P.alias(f1T, [mergedT])
        P.alias(mergedT, [qT])
        for kb in kcb:
            P.alias(kb, [mergedT])
        wbuf = [S("wbuf%d" % i, [128, 8192], BF16) for i in range(2)]
        w1 = P.view("w1v", wbuf[1].ap[0:64, :].rearrange("p (k q h) -> p k q h", k=2, q=32))
        P.alias(wbuf[1], [w1])
        xt = [S("xt%d" % i, [128, D]) for i in range(2)]
        xs = S("xs", [128, D])
        junk = S("junk", [128, 512], BF16)
        ssq = [S("ssq%d" % i, [128, 8]) for i in range(2)]
        tA = [S("tA%d" % i, [128, 512]) for i in range(2)]
        vN = S("vN", [128, 512], BF16)
        macc = tA[0]
        mtmp = tA[1]
        bnst = S("bnst", [128, 8])
        gates = S("gates", [128, 2, 48])
        apb = S("apb", [128, 4, 2, 144])
        apS = [S("apS%d" % i, [128, 2, 144]) for i in range(2)]
        yc = S("yc", [128, 1024])
        class _V:
            def __init__(self, ap): self.ap = ap
        gsb = [_V(yc.ap[:, i * NT:(i + 1) * NT]) for i in range(3)]
        PT = [S("PT%d" % i, [128, 512], BF16) for i in range(3)]
        kch = [S("kch%d" % i, [64, 4, 640], BF16) for i in range(2)]
        vch = [S("vch%d" % i, [128, 5, 4, 66], BF16) for i in range(2)]
        sel = S("sel", [128, 4, 256], BF16)
        selx = [S("selx0", [128, 4, 512], BF16)]
        pooled = P.view("pooled", selx[0].ap[:, 0:2, :].rearrange("p a (b n) -> p (a b) n", n=NT))
        pooled_b = selx[0]
        bon = S("bon", [128, 256], BF16)
        score = S("score", [128, 256])
        swork = S("swork", [128, 256])
        m8 = S("m8", [128, 16])
        zc = S("zc", [128, 16])
        ps = [P.psum("ps%d" % i, [128, 512]) for i in range(8)]

        def mm(out_ap, lhsT, rhs, start, stop, reads, writes):
            P.op("pe", lambda e: e.matmul(out_ap, lhsT=lhsT, rhs=rhs, start=start, stop=stop, skip_group_check=True), reads=reads, writes=writes, wap=out_ap)

        def tr(out_ap, in_ap, idn, reads, writes):
            P.op("pe", lambda e: e.transpose(out=out_ap, in_=in_ap, identity=idn.ap[:, :]), reads=list(reads) + [idn], writes=writes, wap=out_ap)

        def act(out_ap, in_ap, func, reads, writes, **kw):
            P.op("act", lambda e: e.activation(out=out_ap, in_=in_ap, func=func, **kw), reads=reads, writes=writes, wap=(out_ap if 'accum_out' not in kw else None))

        def tt(out_ap, in0, in1, op, reads, writes, eng="dve"):
            P.op(eng, lambda e: e.tensor_tensor(out=out_ap, in0=in0, in1=in1, op=op), reads=reads, writes=writes, wap=out_ap)

        def ts(out_ap, in0, s1, s2, op0, op1, reads, writes):
            if op1 is None:
                P.op("dve", lambda e: e.tensor_scalar(out=out_ap, in0=in0, scalar1=s1, scalar2=None, op0=op0), reads=reads, writes=writes, wap=out_ap)
            else:
                P.op("dve", lambda e: e.tensor_scalar(out=out_ap, in0=in0, scalar1=s1, scalar2=s2, op0=op0, op1=op1), reads=reads, writes=writes, wap=out_ap)

        def stt(out_ap, in0, sc, in1, op0, op1, reads, writes):
            P.op("dve", lambda e: e.scalar_tensor_tensor(out=out_ap, in0=in0, scalar=sc, in1=in1, op0=op0, op1=op1), reads=reads, writes=writes, wap=out_ap)

        def cp(out_ap, in_ap, reads, writes, eng="dve"):
            if eng == "act":
                P.op("act", lambda e: e.copy(out=out_ap, in_=in_ap), reads=reads, writes=writes, wap=out_ap)
            else:
                P.op(eng, lambda e: e.tensor_copy(out=out_ap, in_=in_ap), reads=reads, writes=writes, wap=out_ap)

        gel_i = [0]

        def gelu(out_ap, src_ap, n, p, src_bufs, out_bufs):
            t = tA[gel_i[0] % 2]
            gel_i[0] += 1
            tv = t.ap[0:p, 0:n]
            act(tv, src_ap, AF.Square, src_bufs, [t])
            ts(tv, tv, 0.044715, 1.0, ALU.mult, ALU.add, [t], [t])
            tt(tv, tv, src_ap, ALU.mult, [t] + src_bufs, [t])
            act(tv, tv, AF.Sigmoid, [t], [t], scale=GELU_C)
            tt(out_ap, tv, src_ap, ALU.mult, [t] + src_bufs, out_bufs)

        wb_i = [0]

        def load_w(blk):
            b = wbuf[wb_i[0] % 2]
            wb_i[0] += 1
            P.dma("sp", b.ap[:, :], WT.ap[blk, :, :], b, WT)
            return b

        w16 = lambda b: b.ap.rearrange("p (k c) -> p k c", c=512)
        w64 = lambda b: b.ap.rearrange("p (k c) -> p k c", c=128)
        gb_i = [0]

        def gbank():
            b = ps[gb_i[0] % 4]
            gb_i[0] += 1
            return b

        def rstd_from(ssq_t, col_in, col_out):
            act(ssq_t.ap[:, col_out:col_out + 1], ssq_t.ap[:, col_in:col_in + 1], AF.Sqrt, [ssq_t, eps_b], [ssq_t], scale=1.0 / D, bias=eps_b.ap[:, 0:1])
            P.op("dve", lambda e: e.reciprocal(out=ssq_t.ap[:, col_out:col_out + 1], in_=ssq_t.ap[:, col_out:col_out + 1]), reads=[ssq_t], writes=[ssq_t])

        def ld(dst, src_buf, src_ap=None, q="sp", dst_ap=None):
            P.dma(q, dst.ap if dst_ap is None else dst_ap, src_buf.ap if src_ap is None else src_ap, dst, src_buf)

        ld(ident, identd)
        ld(modT, modTd)
        ld(normgT, normgTd)
        ld(psc, pscd)
        ld(b1, b1d)
        ld(b2k, b2kd)
        ld(cmask, cmaskd)
        ld(cmpm, cmpmd)
        ld(wmask, wmaskd)
        ld(ovb, ovbd)
        ld(invc, invcd)
        ld(lng, lngd, lngd.ap[0:1, :].partition_broadcast(128))
        ld(lnb, lnbd, lnbd.ap[0:1, :].partition_broadcast(128))
        ld(bsb, bsd, bsd.ap[0:1, :].partition_broadcast(128))
        ld(b2v, b2vd, b2vd.ap[0:1, :].partition_broadcast(128))
        ld(wng, wngd, q="pool")
        ld(poolw, poolwd, q="pool")
        ld(posT, posTd, q="pool")
        ld(w1, w1d, q="pool")
        ld(w2, w2d, q="pool")
        P.op("dve", lambda e: e.memset(eps_b.ap[:, :], EPS), writes=[eps_b])
        P.op("dve", lambda e: e.memset(eps5.ap[:, :], 1e-5), writes=[eps5])
        for a_ in apS:
            P.op("dve", lambda e, a_=a_: e.memset(a_.ap[:, :, :], 0.0), writes=[a_])
        cp(identb.ap[:, :], ident.ap[:, :], [ident], [identb])
        stt(G2T.ap[:, :], modT.ap[:, 4, :], 1.0, normgT.ap[:, 2, :], ALU.add, ALU.mult, [modT, normgT], [G2T])
        cp(S2T.ap[:, :], modT.ap[:, 3, :], [modT], [S2T])
        for gi_, (mi, gi) in enumerate(((2, 1), (5, 3))):
            ld(ggrow[0], modrow, modrow.ap[mi:mi + 1, :])
            ld(ggrow[1], normg, normg.ap[gi:gi + 1, :])
            tt(ggrow[0].ap[:, :], ggrow[0].ap[:, :], ggrow[1].ap[:, :], ALU.mult, ggrow, [ggrow[0]])
            P.dma("sp", ggd.ap[gi_:gi_ + 1, :], ggrow[0].ap[:, :], ggd, ggrow[0], own=ggrow[0])
        ld(xs, wsd, dst_ap=xs.ap[:, 0:512].rearrange("p (g j) -> p g j", j=128))
        ld(tA[0], trild, dst_ap=tA[0].ap[:, 0:128])
        for g in range(4):
            tt(xs.ap[:, g * 128:(g + 1) * 128], xs.ap[:, g * 128:(g + 1) * 128], tA[0].ap[:, 0:128], ALU.mult, [xs, tA[0]], [xs])
        for g in range(4):
            tr(ps[4].ap[:, g * 128:(g + 1) * 128], xs.ap[:, g * 128:(g + 1) * 128], ident, [xs], [ps[4]])
        cp(WmT.ap[:, :, :], ps[4].ap[:, :].rearrange("p (g i) -> p g i", i=128), [ps[4]], [WmT])
        P.op("dve", lambda e: e.memset(vcmp.ap[:, :, :, 65:66], 0.0), writes=[vcmp], wap=vcmp.ap[:, :, :, 65:66])
        P.op("dve", lambda e: e.memset(vcmp.ap[:, :, :, 64:65], 1.0), writes=[vcmp], wap=vcmp.ap[:, :, :, 64:65])
        for kind in range(2):
            for p_ in range(32):
                mm(ps[5].ap[:, kind:kind + 1], w1.ap[:, kind, p_, :], posT.ap[:, kind, p_:p_ + 1], p_ == 0 and kind == 0, p_ == 31, [w1, posT], [ps[5]])
        tt(c1.ap[:, :], ps[5].ap[:, 0:2], b1.ap[:, :], ALU.add, [ps[5], b1], [c1])
        kci = 0
        for kind in range(2):
            for g in range(4):
                for half in range(2):
                    kb = kcb[kci % 2]
                    kci += 1
                    P.dma("sp", kb.ap[:, :], KTall.ap[kind, g, :, half * 8192: half * 8192 + 8224], kb, KTall)
                    kv3 = kb.ap[:, :].rearrange("d (n p) -> d n p", p=16)
                    bank = gbank()
                    for p_ in range(32):
                        rhs = kv3[:, (p_ // 16):(p_ // 16) + 512, p_ % 16]
                        mm(bank.ap[:, :], w1.ap[:, kind, p_, :], rhs, p_ == 0, p_ == 31, [w1, kb], [bank])
                    z = tA[gel_i[0] % 2]
                    act(xs.ap[:, 0:512], bank.ap[:, :], AF.Identity, [bank, c1], [xs], bias=c1.ap[:, kind:kind + 1])
                    hid = PT[kci % 3]
                    gelu(hid.ap[:, :], xs.ap[:, 0:512], 512, 128, [xs], [hid])
                    if kind == 0:
                        b2 = gbank()
                        mm(b2.ap[0:64, :], w2.ap[:, 0, :], hid.ap[:, :], True, True, [w2, hid], [b2])
                        act(kcmpT.ap[:, g, half * 512:(half + 1) * 512], b2.ap[0:64, :], AF.Identity, [b2, b2k], [kcmpT], bias=b2k.ap[:, 0:1])
                    else:
                        b2 = gbank()
                        for nt in range(4):
                            mm(b2.ap[:, nt * 64:(nt + 1) * 64], hid.ap[:, nt * 128:(nt + 1) * 128], w2.ap[:, 1, :], nt == 0, nt == 3, [w2, hid], [b2])
                        for nt in range(4):
                            tt(vcmp.ap[:, g, half * 4 + nt, 0:64], b2.ap[:, nt * 64:(nt + 1) * 64], b2v.ap[:, :], ALU.add, [b2, b2v], [vcmp])

        def attention(sl, s):
            qv = lambda g: qT.ap[:, 4 * g:4 * g + 4, sl * 128:(sl + 1) * 128]
            gv = lambda g, b: gates.ap[:, sl, g * 12 + b:g * 12 + 12:3]
            pti = [0]

            def score_tile(kT_ap, K, g, reads):
                bank = ps[pti[0] % 2]
                pt = PT[pti[0] % 3]
                pti[0] += 1
                mm(bank.ap[0:K, :], kT_ap, qv(g), True, True, reads + [qT], [bank])
                act(pt.ap[0:K, :], bank.ap[0:K, :], AF.Exp, [bank], [pt])
                return pt

            def combine(accb, g, b, first):
                zv = accb.ap[:, 0:264].rearrange("p (h c) -> p h c", c=66)[:, :, 64]
                ts(zc.ap[:, 0:4], zv, 1e-30, None, ALU.max, None, [accb], [zc])
                P.op("dve", lambda e: e.reciprocal(out=zc.ap[:, 4:8], in_=zc.ap[:, 0:4]), reads=[zc], writes=[zc])
                tt(zc.ap[:, 8:12], zc.ap[:, 4:8], gv(g, b), ALU.mult, [zc, gates], [zc])
                for h in range(4):
                    o = yc.ap[:, (4 * g + h) * 64:(4 * g + h + 1) * 64]
                    a = accb.ap[:, h * 66:h * 66 + 64]
                    if first:
                        ts(o, a, zc.ap[:, 8 + h:9 + h], None, ALU.mult, None, [accb, zc], [yc])
                    else:
                        stt(o, a, zc.ap[:, 8 + h:9 + h], o, ALU.mult, ALU.add, [accb, zc, yc], [yc])

            P.dma("sp", bon.ap[:, :], bonusd.ap[s, :, :], bon, bonusd)
            nf = s // 2
            for g in range(4):
                accC = ps[4:8]
                tiles = [(i * 128, 128, None) for i in range(nf)]
                if s % 2 == 1:
                    tiles.append((nf * 128, 128, 1))
                else:
                    tiles.append((nf * 128, 64, 0))
                for ti, (n0, K, mk) in enumerate(tiles):
                    pt = score_tile(kcmpT.ap[:, g, n0:n0 + K], K, g, [kcmpT])
                    if mk is not None:
                        pv = pt.ap[0:K, :].rearrange("p (h q) -> p h q", q=128)
                        tt(pv, pv, cmpm.ap[0:K, mk, :].unsqueeze(1).to_broadcast([K, 4, 128]), ALU.mult, [pt, cmpm], [pt])
                    tile_idx = n0 // 128
                    for h in range(4):
                        mm(accC[h].ap[:, 0:65], pt.ap[0:K, h * 128:(h + 1) * 128], vcmp.ap[0:K, g, tile_idx, 0:65], ti == 0, False, [pt, vcmp], [accC[h]])
                        mm(accC[h].ap[:, 128:384], pt.ap[0:K, h * 128:(h + 1) * 128], ovb.ap[0:K, 224 - 32 * tile_idx:480 - 32 * tile_idx], False, ti == len(tiles) - 1, [pt, ovb], [accC[h]])
                for h in range(4):
                    ts(zc.ap[:, h:h + 1], accC[h].ap[:, 64:65], 1e-30, None, ALU.max, None, [accC[h]], [zc])
                P.op("dve", lambda e: e.reciprocal(out=zc.ap[:, 4:8], in_=zc.ap[:, 0:4]), reads=[zc], writes=[zc])
                ts(score.ap[:, :], accC[0].ap[:, 128:384], zc.ap[:, 4:5], None, ALU.mult, None, [accC[0], zc], [score])
                for h in range(1, 4):
                    stt(score.ap[:, :], accC[h].ap[:, 128:384], zc.ap[:, 4 + h:5 + h], score.ap[:, :], ALU.mult, ALU.add, [accC[h], zc, score], [score])
                tt(score.ap[:, :], score.ap[:, :], bon.ap[:, :], ALU.add, [score, bon], [score])
                P.op("dve", lambda e: e.max(out=m8.ap[:, 0:8], in_=score.ap[:, :]), reads=[score], writes=[m8])
                P.op("dve", lambda e: e.match_replace(out=swork.ap[:, :], in_to_replace=m8.ap[:, 0:8], in_values=score.ap[:, :], imm_value=-1e30), reads=[score, m8], writes=[swork])
                P.op("dve", lambda e: e.max(out=m8.ap[:, 8:16], in_=swork.ap[:, :]), reads=[swork], writes=[m8])
                ts(sel.ap[:, g, :], score.ap[:, :], m8.ap[:, 15:16], None, ALU.is_ge, None, [score, m8], [sel])
                tt(zc.ap[:, 8:12], zc.ap[:, 4:8], gv(g, 0), ALU.mult, [zc, gates], [zc])
                for h in range(4):
                    ts(yc.ap[:, (4 * g + h) * 64:(4 * g + h + 1) * 64], accC[h].ap[:, 0:64], zc.ap[:, 8 + h:9 + h], None, ALU.mult, None, [accC[h], zc], [yc])

            nch = 2 * s + 2
            for ch in range(nch):
                kb = kch[ch % 2]
                vb = vch[ch % 2]
                sx = selx[0]
                P.dma("sp", kb.ap[:, :, 0:512], KTall.ap[2, :, :, ch * 512:(ch + 1) * 512].rearrange("g d k -> d g k"), kb, KTall)
                P.dma("sp", vb.ap[:, 0:4, :, :], Vall.ap[ch * 512:(ch + 1) * 512, :, :].rearrange("(t p) g c -> p t g c", p=128), vb, Vall)
                P.op("pool", lambda e, sx=sx, ch=ch: e.tensor_copy(out=sx.ap[:, :, :].rearrange("p g (b k) -> p g b k", k=64),
                                                              in_=sel.ap[:, :, 8 * ch:8 * ch + 8].unsqueeze(3).to_broadcast([128, 4, 8, 64])),
                     reads=[sel], writes=[sx])
                mb = [ps[2].ap[:, 0:512].bitcast(BF16), ps[3].ap[:, 0:512].bitcast(BF16)]
                for g in range(4):
                    for t_ in range(4):
                        tr(mb[g // 2][:, (g % 2) * 512 + t_ * 128:(g % 2) * 512 + (t_ + 1) * 128], sx.ap[:, g, t_ * 128:(t_ + 1) * 128], identb, [sx], [ps[2 + g // 2]])
                diag = ch >= 2 * s
                i0 = (ch - 2 * s) * 4
                for g in range(4):
                    accS = ps[4 + g]
                    for t_ in range(4):
                        pt = score_tile(kb.ap[:, g, t_ * 128:(t_ + 1) * 128], 128, g, [kb])
                        pv = pt.ap[:, :].rearrange("p (h q) -> p h q", q=128)
                        mv = mb[g // 2][:, (g % 2) * 512 + t_ * 128:(g % 2) * 512 + (t_ + 1) * 128]
                        mrd = [ps[2 + g // 2]]
                        tt(pv, pv, mv.unsqueeze(1).to_broadcast([128, 4, 128]), ALU.mult, [pt] + mrd, [pt])
                        if diag:
                            tt(pv, pv, cmask.ap[:, i0 + t_, :].unsqueeze(1).to_broadcast([128, 4, 128]), ALU.mult, [pt, cmask], [pt])
                        for h in range(4):
                            mm(accS.ap[:, h * 66:h * 66 + 65], pt.ap[:, h * 128:(h + 1) * 128], vb.ap[:, t_, g, 0:65],
                               ch == 0 and t_ == 0 and h == 0, ch == nch - 1 and t_ == 3 and h == 3, [pt, vb], [accS])
            for g in range(4):
                combine(ps[4 + g], g, 1, False)

            kb = kch[nch % 2]
            vb = vch[nch % 2]
            P.dma("sp", kb.ap[:, :, :], kwTw.ap[:, :, s, :].rearrange("g d k -> d g k"), kb, kwTw)
            P.dma("sp", vb.ap[:, :, :, :], vww.ap[s, :, :, :].rearrange("(t p) g c -> p t g c", p=128), vb, vww)
            wmi = 0 if s == 0 else 1
            for g in range(4):
                accW = ps[4 + g]
                for r in range(5):
                    pt = score_tile(kb.ap[:, g, r * 128:(r + 1) * 128], 128, g, [kb])
                    pv = pt.ap[:, :].rearrange("p (h q) -> p h q", q=128)
                    tt(pv, pv, wmask.ap[:, wmi, r, :].unsqueeze(1).to_broadcast([128, 4, 128]), ALU.mult, [pt, wmask], [pt])
                    for h in range(4):
                        mm(accW.ap[:, h * 66:h * 66 + 65], pt.ap[:, h * 128:(h + 1) * 128], vb.ap[:, r, g, 0:65],
                           r == 0 and h == 0, r == 4 and h == 3, [pt, vb], [accW])
            for g in range(4):
                combine(ps[4 + g], g, 2, False)
            for b_ in range(2):
                for i in range(4):
                    m = b_ * 4 + i
                    tr(ps[b_].ap[:, i * 128:(i + 1) * 128], yc.ap[:, m * 128:(m + 1) * 128], ident, [yc], [ps[b_]])
                cp(ycT.ap[:, b_ * 4:b_ * 4 + 4, sl * 128:(sl + 1) * 128], ps[b_].ap[:, :].rearrange("p (m q) -> p m q", q=128), [ps[b_]], [ycT], eng="act")

        def norm_residual(sl, tok0, src_fm, gg, xin, xout_store):
            sq = ssq[sl]
            for b_ in range(4):
                bank = ps[4 + b_]
                for i in range(4):
                    m = b_ * 4 + i
                    tr(bank.ap[:, i * 128:(i + 1) * 128], src_fm.ap[:, m, sl * 128:(sl + 1) * 128], ident, [src_fm], [bank])
                act(junk.ap[:, 0:512], bank.ap[:, :], AF.Square, [bank], [junk, sq], accum_out=sq.ap[:, b_:b_ + 1])
            P.op("dve", lambda e: e.tensor_reduce(out=sq.ap[:, 4:5], in_=sq.ap[:, 0:4], axis=AX.X, op=ALU.add), reads=[sq], writes=[sq])
            rstd_from(sq, 4, 5)
            P.dma("sp", xs.ap[:, :], ggd.ap[gg:gg + 1, :].partition_broadcast(128), xs, ggd)
            for b_ in range(4):
                bank = ps[4 + b_]
                stt(xs.ap[:, b_ * 512:(b_ + 1) * 512], bank.ap[:, :], sq.ap[:, 5:6], xs.ap[:, b_ * 512:(b_ + 1) * 512], ALU.mult, ALU.mult, [bank, sq, xs], [xs])
            tt(xin.ap[:, :], xin.ap[:, :], xs.ap[:, :], ALU.add, [xin, xs], [xin])
            if xout_store:
                P.dma("pool", xo.ap[tok0:tok0 + 128, :], xin.ap[:, :], xo, xin, own=xin)

        for G in range(ngrp):
            c0 = G * NT
            P.dma("sp", hT.ap[:, :, :], hTd.ap[:, :, c0:c0 + NT].rearrange("k p n -> p k n"), hT, hTd)
            for sl in range(2):
                P.dma("sp", xt[sl].ap[:, :], x.ap[c0 + sl * 128:c0 + (sl + 1) * 128, :], xt[sl], x)
            wb = load_w(WU)
            for m in range(4):
                bank = gbank()
                for kc in range(16):
                    mm(bank.ap[:, 0:NT], w16(wb)[:, kc, m * 128:(m + 1) * 128], hT.ap[:, kc, :], kc == 0, kc == 15, [wb, hT], [bank])
                gelu(uT.ap[:, m, :], bank.ap[:, 0:NT], NT, 128, [bank], [uT])
            wb = load_w(WV)
            for sl in range(2):
                bank = gbank()
                for kc in range(16):
                    mm(bank.ap[:, :], hT.ap[:, kc, sl * 128:(sl + 1) * 128], w16(wb)[:, kc, :], kc == 0, kc == 15, [wb, hT], [bank])
                gelu(xs.ap[:, 0:512], bank.ap[:, :], 512, 128, [bank], [xs])
                P.op("dve", lambda e: e.bn_stats(out=bnst.ap[:, 0:6], in_=xs.ap[:, 0:512]), reads=[xs], writes=[bnst])
                P.op("dve", lambda e: e.bn_aggr(out=bnst.ap[:, 6:8], in_=bnst.ap[:, 0:6]), reads=[bnst], writes=[bnst])
                act(bnst.ap[:, 7:8], bnst.ap[:, 7:8], AF.Sqrt, [bnst, eps5], [bnst], scale=1.0, bias=eps5.ap[:, 0:1])
                P.op("dve", lambda e: e.reciprocal(out=bnst.ap[:, 7:8], in_=bnst.ap[:, 7:8]), reads=[bnst], writes=[bnst])
                ts(xs.ap[:, 0:512], xs.ap[:, 0:512], bnst.ap[:, 6:7], bnst.ap[:, 7:8], ALU.subtract, ALU.mult, [xs, bnst], [xs])
                tt(xs.ap[:, 0:512], xs.ap[:, 0:512], lng.ap[:, :], ALU.mult, [xs, lng], [xs])
                tt(vN.ap[:, :], xs.ap[:, 0:512], lnb.ap[:, :], ALU.add, [xs, lnb], [vN])
                bank = gbank()
                for g in range(4):
                    mm(bank.ap[:, g * 128:(g + 1) * 128], vN.ap[:, g * 128:(g + 1) * 128], WmT.ap[:, g, :], g == 0, g == 3, [vN, WmT], [bank])
                tt(xs.ap[:, 0:512], bank.ap[:, :], bsb.ap[:, :], ALU.add, [bank, bsb], [xs])
                tt(ybT.ap[:, :, sl * 128:(sl + 1) * 128], xs.ap[:, 0:512].rearrange("p (g i) -> p g i", i=128), uT.ap[:, :, sl * 128:(sl + 1) * 128], ALU.mult, [xs, uT], [ybT])
            for g in range(4):
                P.dma("sp", apb.ap[:, g, :, 16:144], apTd.ap[g, :, c0:c0 + NT].rearrange("c (s t) -> c s t", t=128), apb, apTd)
                P.dma("sp", apb.ap[:, g, :, 0:16], halod.ap[g, :, 2 * G:2 * G + 2, :], apb, halod)
            for g in range(4):
                src = apb.ap[:, g, :, :]
                cur = src
                rd = [apb]
                sh = 1
                for it in range(g + 1):
                    dst = apS[it % 2]
                    tt(dst.ap[:, :, sh:144], cur[:, :, sh:144], cur[:, :, 0:144 - sh], ALU.add, rd, [dst])
                    cur = dst.ap[:, :, :]
                    rd = [dst]
                    sh *= 2
                for sl in range(2):
                    s = 2 * G + sl
                    if s == 0:
                        tt(xs.ap[:, sl * 128:(sl + 1) * 128], cur[:, sl, 16:144], invc.ap[:, g, :], ALU.mult, rd + [invc], [xs])
                    else:
                        ts(xs.ap[:, sl * 128:(sl + 1) * 128], cur[:, sl, 16:144], 1.0 / (2 << g), None, ALU.mult, None, rd, [xs])
                    tt(pooled.ap[:, g, sl * 128:(sl + 1) * 128], xs.ap[:, sl * 128:(sl + 1) * 128], apb.ap[:, g, sl, 16:144], ALU.subtract, [xs, apb], [pooled_b])
                bank = gbank()
                mm(bank.ap[:, 0:NT], poolw.ap[:, g, :], pooled.ap[:, g, :], True, True, [poolw, pooled_b], [bank])
                act(yaT.ap[:, g, :], bank.ap[:, 0:NT], AF.Identity, [bank, psc], [yaT], scale=psc.ap[:, g:g + 1])
            for qb_ in range(2):
                wb = load_w(WQ + qb_)
                for hh in range(8):
                    bank = gbank()
                    for kc in range(16):
                        mm(bank.ap[0:64, 0:NT], w16(wb)[:, kc, hh * 64:(hh + 1) * 64], hT.ap[:, kc, :], kc == 0, kc == 15, [wb, hT], [bank])
                    act(qT.ap[:, qb_ * 8 + hh, :], bank.ap[0:64, 0:NT], AF.Identity, [bank], [qT], scale=0.125)
            for sl in range(2):
                bank = gbank()
                for kc in range(16):
                    mm(bank.ap[:, 0:48], hT.ap[:, kc, sl * 128:(sl + 1) * 128], wng.ap[:, kc, :], kc == 0, kc == 15, [wng, hT], [bank])
                act(gates.ap[:, sl, :], bank.ap[:, 0:48], AF.Sigmoid, [bank], [gates])
            for sl in range(2):
                if do_attn:
                    attention(sl, 2 * G + sl)
                else:
                    P.op("dve", lambda e, sl=sl: e.memset(ycT.ap[:, :, sl * 128:(sl + 1) * 128], 0.0), writes=[ycT])
            for m in range(16):
                wb = load_w(WM + m)
                wv = w64(wb)
                for b_ in range(3):
                    bank = gbank()
                    for kc in range(16):
                        mm(bank.ap[:, 0:NT], wv[:, b_ * 16 + kc, :], hT.ap[:, kc, :], kc == 0, kc == 15, [wb, hT], [bank])
                    act(gsb[b_].ap[:, :], bank.ap[:, 0:NT], AF.Sigmoid, [bank], [yc])
                for b_, (src, nk, k0) in enumerate(((yaT, 4, 48), (ybT, 4, 52), (ycT, 8, 56))):
                    bank = gbank()
                    for kc in range(nk):
                        mm(bank.ap[:, 0:NT], wv[:, k0 + kc, :], src.ap[:, kc, :], kc == 0, kc == nk - 1, [wb, src], [bank])
                    if b_ == 0:
                        tt(macc.ap[:, 0:NT], bank.ap[:, 0:NT], gsb[0].ap[:, :], ALU.mult, [bank, yc], [macc])
                    else:
                        tt(mtmp.ap[:, 0:NT], bank.ap[:, 0:NT], gsb[b_].ap[:, :], ALU.mult, [bank, yc], [mtmp])
                        if b_ == 1:
                            tt(macc.ap[:, 0:NT], macc.ap[:, 0:NT], mtmp.ap[:, 0:NT], ALU.add, [macc, mtmp], [macc])
                        else:
                            tt(mergedT.ap[:, m, :], macc.ap[:, 0:NT], mtmp.ap[:, 0:NT], ALU.add, [macc, mtmp], [mergedT])
            for blk in range(4):
                wb = load_w(WO + blk)
                for mi in range(4):
                    bank = gbank()
                    for kc in range(16):
                        mm(bank.ap[:, 0:NT], w16(wb)[:, kc, mi * 128:(mi + 1) * 128], mergedT.ap[:, kc, :], kc == 0, kc == 15, [wb, mergedT], [bank])
                    cp(yfT.ap[:, blk * 4 + mi, :], bank.ap[:, 0:NT], [bank], [yfT], eng="act")
            for sl in range(2):
                norm_residual(sl, c0 + sl * 128, yfT, 0, xt[sl], False)
            for sl in range(2):
                sq = ssq[sl]
                for b_ in range(4):
                    act(junk.ap[:, :], xt[sl].ap[:, b_ * 512:(b_ + 1) * 512], AF.Square, [xt[sl]], [junk, sq], accum_out=sq.ap[:, b_:b_ + 1])
                P.op("dve", lambda e, sq=sq: e.tensor_reduce(out=sq.ap[:, 6:7], in_=sq.ap[:, 0:4], axis=AX.X, op=ALU.add), reads=[sq], writes=[sq])
                rstd_from(sq, 6, 7)
                ts(xs.ap[:, :], xt[sl].ap[:, :], sq.ap[:, 7:8], None, ALU.mult, None, [xt[sl], sq], [xs])
                for b_ in range(4):
                    bank = ps[4 + b_]
                    for i in range(4):
                        kc = b_ * 4 + i
                        tr(bank.ap[:, i * 128:(i + 1) * 128], xs.ap[:, kc * 128:(kc + 1) * 128], ident, [xs], [bank])
                    for i in range(4):
                        kc = b_ * 4 + i
                        act(hT.ap[:, kc, sl * 128:(sl + 1) * 128], bank.ap[:, i * 128:(i + 1) * 128], AF.Identity, [bank, G2T, S2T], [hT],
                            scale=G2T.ap[:, kc:kc + 1], bias=S2T.ap[:, kc:kc + 1])
            for blk in range(16):
                wb = load_w(WF1 + blk)
                for mi in range(4):
                    bank = gbank()
                    for kc in range(16):
                        mm(bank.ap[:, 0:NT], w16(wb)[:, kc, mi * 128:(mi + 1) * 128], hT.ap[:, kc, :], kc == 0, kc == 15, [wb, hT], [bank])
                    t = tA[(blk * 4 + mi) % 2]
                    act(t.ap[:, 0:NT], bank.ap[:, 0:NT], AF.Relu, [bank], [t])
                    tt(f1T.ap[:, blk * 4 + mi, :], t.ap[:, 0:NT], t.ap[:, 0:NT], ALU.mult, [t], [f1T])
            for m in range(16):
                wb = load_w(WF2 + m)
                bank = gbank()
                for kc in range(64):
                    mm(bank.ap[:, 0:NT], w64(wb)[:, kc, :], f1T.ap[:, kc, :], kc == 0, kc == 63, [wb, f1T], [bank])
                cp(yfT.ap[:, m, :], bank.ap[:, 0:NT], [bank], [yfT], eng="act")
            for sl in range(2):
                norm_residual(sl, c0 + sl * 128, yfT, 1, xt[sl], True)
        P.finish([xo])
        print("phase B instructions:", P.n_inst)
    return nc

import ml_dtypes
BF = ml_dtypes.bfloat16
POOL_WINDOWS = (2, 4, 8, 16)
TP = 16384 + 128


def build_M():
    nc = bass.Bass("TRN2", target_bir_lowering=False)
    with ExitStack() as es:
        P = Prog(nc, es)
        cT = P.dram("cT", [128, 16], F32, "ExternalInput")
        w = P.dram("w", [2, 2048, 1536], F32, "ExternalInput")
        b = P.dram("b", [2, 1536], F32, "ExternalInput")
        o = P.dram("o", [2, 1536], F32, "ExternalOutput")
        c_sb = P.sbuf("c_sb", [128, 16], F32)
        ca = P.sbuf("ca", [128, 16], F32)
        b_sb = P.sbuf("b_sb", [1, 2, 1536], F32)
        o_sb = P.sbuf("o_sb", [1, 2, 1536], F32)
        wb = [P.sbuf("wb%d" % i, [128, 16, 512], F32) for i in range(2)]
        ps = [P.psum("ps%d" % i, [128, 512]) for i in range(2)]
        P.dma("sp", c_sb.ap[:, :], cT.ap[:, :], c_sb, cT)
        P.dma("sp", b_sb.ap[:, :, :], b.ap[:, :].unsqueeze(0), b_sb, b)
        P.op("act", lambda e: e.activation(out=ca.ap[:, :], in_=c_sb.ap[:, :], func=AF.Silu), reads=[c_sb], writes=[ca])
        i = 0
        for l in range(2):
            for n in range(3):
                wt = wb[i % 2]
                bank = ps[i % 2]
                i += 1
                for kq in range(4):
                    P.dma("sp", wt.ap[:, kq * 4:(kq + 1) * 4, :], w.ap[l, kq * 512:(kq + 1) * 512, n * 512:(n + 1) * 512].rearrange("(k p) c -> p k c", p=128), wt, w)
                for kc in range(16):
                    P.op("pe", lambda e, kc=kc, wt=wt, bank=bank: e.matmul(bank.ap[0:1, :], lhsT=ca.ap[:, kc:kc + 1], rhs=wt.ap[:, kc, :], start=(kc == 0), stop=(kc == 15)),
                         reads=[ca, wt], writes=[bank])
                P.op("dve", lambda e, l=l, n=n, bank=bank: e.tensor_tensor(out=o_sb.ap[0:1, l, n * 512:(n + 1) * 512], in0=bank.ap[0:1, :], in1=b_sb.ap[0:1, l, n * 512:(n + 1) * 512], op=ALU.add),
                     reads=[bank, b_sb], writes=[o_sb])
        P.dma("sp", o.ap[:, :].unsqueeze(0), o_sb.ap[:, :, :], o, o_sb)
        P.finish([o])
    return nc


def core_rows(a, c):
    return np.ascontiguousarray(a.reshape(16, 8, 128, *a.shape[1:])[:, c].reshape(2048, *a.shape[1:]))


def uncore_rows(parts):
    a = np.stack(parts, 0)
    a = a.reshape(8, 16, 128, *a.shape[2:])
    a = np.moveaxis(a, 0, 1)
    return np.ascontiguousarray(a.reshape(16384, *a.shape[3:]))


def uncore_last(parts):
    a = np.stack(parts, -2)
    a = a.reshape(*a.shape[:-1], 16, 128)
    a = np.moveaxis(a, -3, -2)
    return np.ascontiguousarray(a.reshape(*a.shape[:-3], 16384))


def tile16(W):
    return W.reshape(16, 128, 512).transpose(1, 0, 2).reshape(128, 8192)


def tile64(W):
    return W.reshape(64, 128, 128).transpose(1, 0, 2).reshape(128, 8192)


def weight_blocks(inp, l):
    w_in = inp['w_in'][l]
    blocks = []
    blocks.append(tile16(w_in[:, 512:1024]))
    blocks.append(tile16(w_in[:, 1024:1536]))
    blocks.append(tile16(w_in[:, 1536:2048]))
    blocks.append(tile16(w_in[:, 2048:2560]))
    bg = w_in[:, 4144:10288]
    for m in range(16):
        cs = slice(m * 128, (m + 1) * 128)
        rows = np.concatenate([bg[:, 0 * 2048 + m * 128:0 * 2048 + (m + 1) * 128], bg[:, 2048 + m * 128:2048 + (m + 1) * 128], bg[:, 4096 + m * 128:4096 + (m + 1) * 128],
                               inp['w_br_pool'][l][:, cs], inp['w_br_gmlp'][l][:, cs], inp['w_br_nsa'][l][:, cs]], axis=0)
        blocks.append(tile64(rows))
    for b in range(4):
        blocks.append(tile16(inp['w_out'][l][:, b * 512:(b + 1) * 512]))
    for b in range(16):
        blocks.append(tile16(inp['w_ff1'][l][:, b * 512:(b + 1) * 512]))
    for m in range(16):
        blocks.append(tile64(inp['w_ff2'][l][:, m * 128:(m + 1) * 128]))
    return np.ascontiguousarray(np.stack(blocks, 0), dtype=np.float32)


def core_consts(c):
    k = np.arange(128)[:, None]
    q = np.arange(128)[None, :]
    cmask = np.zeros((128, 8, 128), np.float32)
    for i in range(8):
        if i < c:
            cmask[:, i, :] = 1
        elif i == c:
            cmask[:, i, :] = (k <= q)
    nl = np.arange(64)[:, None]
    cm = (16 * nl + 31 <= 128 * c + q).astype(np.float32)
    cmpm = np.zeros((128, 2, 128), np.float32)
    cmpm[0:64, 0] = cm
    cmpm[0:64, 1] = 1
    cmpm[64:128, 1] = cm
    wm = np.zeros((128, 2, 5, 128), np.float32)
    for r in range(5):
        base = (k > q) if r == 0 else ((k <= q) if r == 4 else np.ones((128, 128)))
        wm[:, 1, r] = base
        wm[:, 0, r] = base * (1.0 if c - 4 + r >= 0 else 0.0)
    bonus = np.zeros((16, 128, 256), np.float32)
    for s in range(16):
        qb = 8 * s + c
        cur = 2 * qb + (np.arange(128) >= 64)
        bonus[s, :, 0] = 1000
        bonus[s, np.arange(128), cur] = 1000
        ok = cur - 1 >= 0
        bonus[s, np.arange(128)[ok], (cur - 1)[ok]] = 1000
    ovb = np.zeros((128, 480), np.float32)
    for n in range(128):
        ovb[n, 224 + n // 4] = 1
        if n % 4 == 3:
            ovb[n, 225 + n // 4] = 1
    invc = np.zeros((128, 4, 128), np.float32)
    t = np.arange(128)
    for g, w in enumerate(POOL_WINDOWS):
        invc[:, g, :] = (1.0 / np.minimum(t + 1, w)) if c == 0 else 1.0 / w
    return dict(cmask=cmask.astype(BF), cmpm=cmpm.astype(BF), wmask=wm.astype(BF), bonus=bonus.astype(BF), ovb=ovb.astype(BF), invc=invc,
                ident=np.eye(128, dtype=np.float32), tril=np.tril(np.ones((128, 128), np.float32)))


def layer_small_inputs(inp, l):
    d = {}
    d['wng'] = np.ascontiguousarray(inp['w_in'][l][:, 4096:4144].reshape(16, 128, 48).transpose(1, 0, 2))
    d['poolw'] = np.ascontiguousarray(inp['pool_w'][l].transpose(1, 0, 2))
    d['pscT'] = np.ascontiguousarray(inp['pool_scale'][l].reshape(4, 128).T)
    d['lng'] = inp['gmlp_ln_g'][l].reshape(1, 512)
    d['lnb'] = inp['gmlp_ln_b'][l].reshape(1, 512)
    d['ws'] = np.ascontiguousarray(inp['gmlp_ws'][l].transpose(1, 0, 2))
    d['bs'] = inp['gmlp_bs'][l].reshape(1, 512)
    d['posT'] = np.ascontiguousarray(inp['cmp_pos'][l].transpose(2, 0, 1))
    d['w1t'] = np.ascontiguousarray(inp['cmp_w1'][l].reshape(2, 32, 64, 128).transpose(2, 0, 1, 3))
    d['b1T'] = np.ascontiguousarray(inp['cmp_b1'][l].T)
    d['w2'] = np.ascontiguousarray(inp['cmp_w2'][l].transpose(1, 0, 2))
    d['b2kT'] = np.ascontiguousarray(inp['cmp_b2'][l][0].reshape(64, 1))
    d['b2v'] = np.ascontiguousarray(inp['cmp_b2'][l][1].reshape(1, 64))
    return d


def modT_of(mod_l):
    return np.ascontiguousarray(mod_l.reshape(6, 16, 128).transpose(2, 0, 1))


def normgT_of(ng):
    return np.ascontiguousarray(ng.reshape(4, 16, 128).transpose(2, 0, 1))


def make_B_inputs(inp, l, mod_l, x_cores, resA, consts):
    KT = uncore_last([np.asarray(r['KT_o']) for r in resA])
    Vg = uncore_rows([np.asarray(r['V_o']).transpose(1, 0, 2) for r in resA])
    apT = uncore_last([np.asarray(r['apT_o']) for r in resA])
    WTb = np.concatenate([np.asarray(r['wdst']) for r in resA], 0)
    KTall = np.zeros((3, 4, 64, TP), BF)
    KTall[:, :, :, :16384] = KT[[0, 1, 2]]
    Vall = np.zeros((16384, 4, 66), BF)
    Vall[:, :, :64] = Vg[:, 0].reshape(16384, 4, 64)
    Vall[:, :, 64] = 1
    kwpad = np.zeros((4, 64, 512 + 16384), BF)
    kwpad[:, :, 512:] = KT[3]
    vwpad = np.zeros((512 + 16384, 4, 66), BF)
    vwpad[512:, :, :64] = Vg[:, 1].reshape(16384, 4, 64)
    vwpad[512:, :, 64] = 1
    appad = np.zeros((4, 128, 16 + 16384), np.float32)
    appad[:, :, 16:] = apT
    small = layer_small_inputs(inp, l)
    maps = []
    for c in range(8):
        kw = np.zeros((4, 64, 16, 640), BF)
        vw = np.zeros((16, 640, 4, 66), BF)
        halo = np.zeros((4, 128, 16, 16), np.float32)
        for s in range(16):
            qb = 8 * s + c
            kw[:, :, s, :] = kwpad[:, :, 128 * qb:128 * qb + 640]
            vw[s] = vwpad[128 * qb:128 * qb + 640]
            halo[:, :, s, :] = appad[:, :, 128 * qb:128 * qb + 16]
        m = dict(x=x_cores[c], hT=np.asarray(resA[c]['hT_o']), apT=np.asarray(resA[c]['apT_o']), halo=halo, modT=modT_of(mod_l), modrow=np.ascontiguousarray(mod_l),
                 normg=np.ascontiguousarray(inp['norm_g'][l]), normgT=normgT_of(inp['norm_g'][l]), KTall=KTall, Vall=Vall, kwTw=kw, vww=vw, WT=WTb)
        m.update(small)
        m.update({k: v for k, v in consts[c].items()})
        maps.append(m)
    return maps


def make_A_inputs(inp, l, mod_l, x_cores, wblocks):
    w_in = inp['w_in'][l]
    wA = np.ascontiguousarray(np.concatenate([w_in[:, 0:512], w_in[:, 2560:4096]], axis=1))
    maps = []
    for c in range(8):
        maps.append(dict(x=x_cores[c], modT=modT_of(mod_l), normgT=normgT_of(inp['norm_g'][l]), wA=wA, ident=np.eye(128, dtype=np.float32),
                         wsrc=np.ascontiguousarray(wblocks[7 * c:7 * c + 7])))
    return maps


from concourse.bass_utils import run_bass_kernel_spmd

_PROGS = {}


def _prog(name, fn):
    if name not in _PROGS:
        _PROGS[name] = fn()
    return _PROGS[name]


def kernel(**inputs):
    inp = {k: np.asarray(v) for k, v in inputs.items()}
    cores = list(range(8))
    x = np.ascontiguousarray(inp['x'][0], dtype=np.float32)
    ncM = _prog("M", build_M)
    cT = np.ascontiguousarray(inp['c'][0].reshape(16, 128).T)
    mapsM = [dict(cT=cT, w=np.ascontiguousarray(inp['w_ada'][:, :, 1536 * c:1536 * (c + 1)]), b=np.ascontiguousarray(inp['b_ada'][:, 1536 * c:1536 * (c + 1)])) for c in cores]
    resM = run_bass_kernel_spmd(ncM, mapsM, core_ids=cores).results
    mod = np.concatenate([np.asarray(r['o']) for r in resM], axis=1).reshape(2, 6, 2048)
    consts = [core_consts(c) for c in cores]
    ncA = _prog("A", build_A)
    ncB = _prog("B", build_B)
    x_cores = [core_rows(x, c) for c in cores]
    for l in range(2):
        wblocks = weight_blocks(inp, l)
        resA = run_bass_kernel_spmd(ncA, make_A_inputs(inp, l, mod[l], x_cores, wblocks), core_ids=cores).results
        del wblocks
        resA = [{k: np.asarray(v) for k, v in r.items()} for r in resA]
        mapsB = make_B_inputs(inp, l, mod[l], x_cores, resA, consts)
        resB = run_bass_kernel_spmd(ncB, mapsB, core_ids=cores).results
        x_cores = [np.ascontiguousarray(np.asarray(r['xo']), dtype=np.float32) for r in resB]
    out = uncore_rows(x_cores)
    return out.reshape(1, 16384, 2048).astype(np.float32)
```

```python
from contextlib import ExitStack
import numpy as np
import concourse.bass as bass
import concourse.mybir as mybir

F32 = mybir.dt.float32
BF16 = mybir.dt.bfloat16
AF = mybir.ActivationFunctionType
ALU = mybir.AluOpType
AX = mybir.AxisListType


class Buf:
    def __init__(self, name, ap=None):
        self.name = name
        self.ap = ap
        self.w = {}
        self.r = {}
        self.sem = None
        self.cnt = 0
        self.aliases = []
        self.wl = []

    def __getitem__(self, idx):
        return self.ap[idx]


class Prog:
    ENG = ("pe", "act", "dve", "pool", "sp")

    def __init__(self, nc, es):
        self.nc = nc
        self.es = es
        self.streams = {e: [] for e in self.ENG}
        self.cnt = {e: 0 for e in self.ENG}
        self.known = {e: {} for e in self.ENG}
        self.sems = {}
        for e in ("pe", "act", "dve", "pool"):
            self.sems[e] = es.enter_context(nc.semaphore("sem_" + e))
        self.nsem = 4
        self.dma_bufs = []
        self.dcnt = {}
        self.n_inst = 0

    def sbuf(self, name, shape, dt):
        t = self.es.enter_context(self.nc.sbuf_tensor(name, list(shape), dt))
        return Buf(name, t[tuple(slice(None) for _ in shape)])

    def psum(self, name, shape, dt=F32):
        t = self.es.enter_context(self.nc.psum_tensor(name, list(shape), dt))
        return Buf(name, t[tuple(slice(None) for _ in shape)])

    def dram(self, name, shape, dt, kind="Internal"):
        t = self.nc.dram_tensor(name, list(shape), dt, kind=kind)
        return Buf(name, t.ap())

    def _dsem(self, buf, q):
        cls = "sw" if q == "pool" else "hw"
        key = (id(buf), cls)
        if key not in self.sems:
            self.sems[key] = self.es.enter_context(self.nc.semaphore("dsem_%d_%s_%s" % (self.nsem, buf.name, cls)))
            self.nsem += 1
            self.dcnt[key] = 0
            if buf not in self.dma_bufs:
                self.dma_bufs.append(buf)
        return key

    def _deps(self, eng, reads, writes, dma_dst=None):
        need = {}

        def add(k, v, kind):
            if not isinstance(k, str) or k == "cc":
                v = max(v, self.dcnt.get(k, 0))
            if k == eng:
                if eng == "pe" or kind == "waw":
                    return
            if need.get(k, 0) < v:
                need[k] = v

        for b in reads:
            for k, v in b.w.items():
                add(k, v, "raw")
        for b in writes:
            for k, v in b.w.items():
                if dma_dst is b and not isinstance(k, str):
                    continue
                add(k, v, "waw")
            for k, v in b.r.items():
                add(k, v, "war")
        waits = []
        kn = self.known[eng]
        for k, v in need.items():
            if kn.get(k, 0) < v:
                kn[k] = v
                waits.append((k, v))
        return waits

    @staticmethod
    def _expand(bufs):
        out = []
        for b in bufs:
            if b not in out:
                out.append(b)
            for a in b.aliases:
                if a not in out:
                    out.append(a)
        return out

    def alias(self, big, smalls):
        for s_ in smalls:
            big.aliases.append(s_)
            s_.aliases.append(big)

    def view(self, name, ap):
        return Buf(name, ap)

    @staticmethod
    def _box(ap):
        try:
            t = ap.tensor
            shp = list(t.shape)
            per = 1
            for d in shp[1:]:
                per *= int(d)
            off = int(ap.offset)
            dims = [(int(a), int(b)) for a, b in ap.ap]
            p0 = off // per
            f0 = off % per
            pst, pc = dims[0]
            p1 = p0 + (pc - 1) * (pst // per if per else 0)
            f1 = f0 + sum(abs(st) * (c - 1) for st, c in dims[1:])
            return (t.name, p0, p1, f0, f1)
        except Exception:
            return None

    @staticmethod
    def _overlap(a, b):
        if a is None or b is None:
            return True
        if a[0] != b[0]:
            return False
        return not (a[2] < b[1] or b[2] < a[1] or a[4] < b[3] or b[4] < a[3])

    def op(self, eng, fn, reads=(), writes=(), wap=None):
        reads = self._expand(reads)
        writes = self._expand(writes)
        waits = self._deps(eng, reads, writes)
        box = self._box(wap) if wap is not None else None
        if eng != "pe":
            kn = self.known[eng]
            need = 0
            for b in writes:
                for (e2, c2, bx) in b.wl:
                    if e2 == eng and c2 > kn.get(eng, 0) and self._overlap(box, bx):
                        need = max(need, c2)
            if need > kn.get(eng, 0):
                kn[eng] = need
                waits.append((eng, need))
        self.cnt[eng] += 1
        ev = (eng, self.cnt[eng])
        for b in writes:
            b.wl.append((eng, ev[1], box))
            if len(b.wl) > 12:
                b.wl = b.wl[-12:]
        self.streams[eng].append((waits, fn, ev, 1))
        for b in reads:
            if b.r.get(eng, 0) < ev[1]:
                b.r[eng] = ev[1]
        for b in writes:
            b.w = {eng: ev[1]}
            b.r = {}
        self.n_inst += 1

    def dma(self, q, out_ap, in_ap, dst, src, own=None, **kw):
        if own is None:
            own = dst
        key = self._dsem(own, q)
        srcs = self._expand([src])
        dsts = self._expand([dst])
        waits = self._deps(q, srcs, dsts, dma_dst=dst)
        self.dcnt[key] += 16
        ev = (key, self.dcnt[key])
        self.streams[q].append((waits, lambda e: e.dma_start(out=out_ap, in_=in_ap, **kw), ev, 16))
        for s_ in srcs:
            if s_.r.get(ev[0], 0) < ev[1]:
                s_.r[ev[0]] = ev[1]
        for d_ in dsts:
            if any(isinstance(k, str) for k in d_.w):
                d_.w = {}
            d_.w[ev[0]] = ev[1]
            d_.r = {}
        self.n_inst += 1

    def cc_allgather(self, src, dst):
        if "cc" not in self.sems:
            self.sems["cc"] = self.es.enter_context(self.nc.semaphore("sem_cc"))
            self.dcnt["cc"] = 0
        waits = self._deps("pool", self._expand([src]), self._expand([dst]))
        self.dcnt["cc"] += 1
        ev = ("cc", self.dcnt["cc"])
        sap, dap = src.ap, dst.ap
        self.streams["pool"].append((waits, lambda e: e.collective_compute("AllGather", ALU.bypass, replica_groups=[list(range(8))], ins=[sap.opt()], outs=[dap.opt()]), ev, 1))
        src.r["cc"] = ev[1]
        dst.w = {"cc": ev[1]}
        dst.r = {}
        self.n_inst += 1

    def finish(self, out_bufs):
        nc = self.nc
        fw = {}
        for b in list(out_bufs) + self.dma_bufs:
            for k, v in b.w.items():
                if not isinstance(k, str) and fw.get(k, 0) < v:
                    fw[k] = v
        for key, v in self.dcnt.items():
            if fw.get(key, 0) < v:
                fw[key] = v
        final_waits = list(fw.items())
        engmap = {"pe": "tensor", "act": "scalar", "dve": "vector", "pool": "gpsimd", "sp": "sync"}
        with nc.Block() as block:
            for e in self.ENG:
                stream = self.streams[e]
                extra = final_waits if e == "sp" else []

                def body(eng, stream=stream, extra=extra):
                    for waits, fn, ev, inc in stream:
                        for k, v in waits:
                            eng.wait_ge(self.sems[k], v)
                        inst = fn(eng)
                        inst.then_inc(self.sems[ev[0]], inc)
                    for k, v in extra:
                        eng.wait_ge(self.sems[k], v)

                getattr(block, engmap[e])(body)


T = 2048
D = 2048
NT = 256
NGRP = 8
EPS = 1e-6
GELU_C = 1.5957691216057308
WU, WV, WQ, WM, WO, WF1, WF2, WA = 0, 1, 2, 4, 20, 24, 40, 56
NBLK = 60
WG_START = [0, 4, 20, 24, 40, 56]
WG_SIZE = [4, 16, 4, 16, 16, 4]


def wgroup_of(blk):
    for i in range(5, -1, -1):
        if blk >= WG_START[i]:
            return i


def build_F(NL=2, ngrp=NGRP, ngrpB=None):
    nc = bass.Bass("TRN2", target_bir_lowering=False)
    with ExitStack() as es:
        P = Prog(nc, es)
        LCUR = [0]
        I = lambda name, shape, dt=F32: P.dram(name, shape, dt, "ExternalInput")
        x = I("x", [T, D])
        cTd = I("cT", [128, 16])
        wadad = I("wada", [2, D, 1536])
        badad = I("bada", [2, 1536])
        normgd = I("normg", [2, 4, D])
        normgTd = I("normgT", [128, 2, 4, 16])
        wsrc = I("wsrc", [2, NBLK, 128, 8192])
        wngd = I("wng", [2, 128, 16, 48])
        poolwd = I("poolw", [2, 128, 4, 128])
        pscd = I("pscT", [2, 128, 4])
        lngd = I("lng", [2, 512])
        lnbd = I("lnb", [2, 512])
        wsd = I("ws", [2, 128, 4, 128])
        bsd = I("bs", [2, 512])
        trild = I("tril", [128, 128])
        posTd = I("posT", [2, 64, 2, 32])
        w1d = I("w1t", [2, 64, 2, 32, 128])
        b1d = I("b1T", [2, 128, 2])
        w2d = I("w2", [2, 128, 2, 64])
        b2kd = I("b2kT", [2, 64, 1])
        b2vd = I("b2v", [2, 64])
        identd = I("ident", [128, 128])
        cmaskd = I("cmask", [128, 8, 128], BF16)
        cmpmd = I("cmpm", [128, 2, 128], BF16)
        wmaskd = I("wmask12", [128, 12, 128], BF16)
        bonusd = I("bonus", [16, 128, 256], BF16)
        ovbd = I("ovb", [128, 480], BF16)
        invcd = I("invc", [128, 4, 128])
        hseld = I("hsel", [128, 9])
        xo = P.dram("xo", [T, D], F32, "ExternalOutput")
        WTb = [[P.dram("WTb_%d_%d" % (l, g), [WG_SIZE[g], 128, 8192], BF16) for g in range(6)] for l in range(2)]
        modx = P.dram("modx", [2, 1536], F32)
        modg = P.dram("modg", [16, 1536], F32)
        KTx = [P.dram("KTx%d" % l, [1024, T], BF16) for l in range(2)]
        KTg = [P.dram("KTg%d" % l, [8192, T], BF16) for l in range(2)]
        Vx = [P.dram("Vx%d" % l, [4096, 264], BF16) for l in range(2)]
        Vg = [P.dram("Vg%d" % l, [32768, 264], BF16) for l in range(2)]
        Hx = [P.dram("Hx%d" % l, [512, 256], F32) for l in range(2)]
        Hg = [P.dram("Hg%d" % l, [4096, 256], F32) for l in range(2)]
        hTs = P.dram("hTs", [16, 128, T], BF16)
        apTs = P.dram("apTs", [4, 128, T], F32)
        xcur = P.dram("xcur", [T, D], F32)
        ggd = P.dram("ggd", [2, D], F32)

        S = lambda name, shape, dt=F32: P.sbuf(name, shape, dt)
        ident = S("ident_sb", [128, 128])
        identb = S("identb", [128, 128], BF16)
        modT = S("modT_sb", [128, 6, 16])
        normgT = S("normgT_sb", [128, 2, 4, 16])
        G1T = S("G1T", [128, 16])
        S1T = S("S1T", [128, 16])
        G2T = S("G2T", [128, 16])
        S2T = S("S2T", [128, 16])
        eps_b = S("eps", [128, 1])
        eps5 = S("eps5", [128, 1])
        ggrow = [S("ggrow%d" % i, [1, D]) for i in range(2)]
        wng = S("wng_sb", [128, 16, 48], BF16)
        poolw = S("poolw_sb", [128, 4, 128], BF16)
        psc = S("psc", [128, 4])
        lng = S("lng_sb", [128, 512])
        lnb = S("lnb_sb", [128, 512])
        bsb = S("bsb", [128, 512])
        WmT = S("WmT", [128, 4, 128], BF16)
        posT = S("posT_sb", [64, 2, 32], BF16)
        b1 = S("b1_sb", [128, 2])
        c1 = S("c1", [128, 2])
        w2 = S("w2_sb", [128, 2, 64], BF16)
        b2k = S("b2k", [128, 1])
        w2k2 = S("w2k2", [128, 128], BF16)
        b2v = S("b2v_sb", [128, 64])
        cmask = S("cmask_sb", [128, 8, 128], BF16)
        cmpm = S("cmpm_sb", [128, 2, 128], BF16)
        wmask = S("wmask_sb", [128, 12, 128], BF16)
        ovb = S("ovb_sb", [128, 480], BF16)
        invc = S("invc_sb", [128, 4, 128])
        hsel = S("hsel_sb", [128, 9])
        kcmpT = S("kcmpT", [128, 4, 1024], BF16)
        vcmp = S("vcmp", [128, 4, 8, 66], BF16)
        arena = S("arena", [128, 16512], BF16)
        f1T = P.view("f1T", arena.ap[:, 0:16384].rearrange("p (m n) -> p m n", n=NT))
        qT = P.view("qT", arena.ap[:, 0:4096].rearrange("p (h n) -> p h n", n=NT))
        uT = P.view("uT", arena.ap[:, 4096:5120].rearrange("p (m n) -> p m n", n=NT))
        ybT = P.view("ybT", arena.ap[:, 5120:6144].rearrange("p (m n) -> p m n", n=NT))
        yaT = P.view("yaT", arena.ap[:, 6144:7168].rearrange("p (m n) -> p m n", n=NT))
        ycT = P.view("ycT", arena.ap[:, 7168:9216].rearrange("p (m n) -> p m n", n=NT))
        kcb = [P.view("kcb0", arena.ap[0:64, 0:8224]), P.view("kcb1", arena.ap[0:64, 8224:16448])]
        smalls = [qT, uT, ybT, yaT, ycT]
        P.alias(f1T, smalls)
        for kb in kcb:
            P.alias(kb, smalls + [f1T])
        yfT = S("yfT", [128, 16, NT])
        hT = S("hT_sb", [128, 16, NT], BF16)
        mergedT = P.view("mergedT", arena.ap[:, 0:4096].rearrange("p (m n) -> p m n", n=NT))
        P.alias(f1T, [mergedT])
        P.alias(mergedT, [qT])
        for kb in kcb:
            P.alias(kb, [mergedT])
        wbuf = [S("wbuf%d" % i, [128, 8192], BF16) for i in range(2)]
        w1 = P.view("w1v", wbuf[1].ap[0:64, :].rearrange("p (k q h) -> p k q h", k=2, q=32))
        P.alias(wbuf[1], [w1])
        xt = [S("xt%d" % i, [128, D]) for i in range(2)]
        xs = S("xs", [128, D])
        junk = S("junk", [128, 512], BF16)
        ssq = [S("ssq%d" % i, [128, 8]) for i in range(2)]
        tA = [S("tA%d" % i, [128, 512]) for i in range(2)]
        vN = S("vN", [128, 512], BF16)
        macc = tA[0]
        mtmp = tA[1]
        bnst = S("bnst", [128, 8])
        gates = S("gates", [128, 2, 48])
        apb = S("apb", [128, 4, 2, 144])
        apS = [S("apS%d" % i, [128, 2, 144]) for i in range(2)]
        cand = S("cand", [128, 4, 9, 16])
        yc = S("yc", [128, 1024])
        class _V:
            def __init__(self, ap): self.ap = ap
        gsb = [_V(yc.ap[:, i * NT:(i + 1) * NT]) for i in range(3)]
        PT = [S("PT%d" % i, [128, 512], BF16) for i in range(4)]
        kch = [S("kch%d" % i, [128, 4, 512], BF16) for i in range(2)]
        vch = [S("vch%d" % i, [128, 4, 4, 66], BF16) for i in range(2)]
        vstg = [S("vstg%d" % i, [128, 4, 66], BF16) for i in range(2)]
        sel = S("sel", [128, 4, 256], BF16)
        selx = [S("selx0", [128, 4, 512], BF16)]
        pooled = P.view("pooled", selx[0].ap[:, 0:2, :].rearrange("p a (b n) -> p (a b) n", n=NT))
        pooled_b = selx[0]
        bon = S("bon", [128, 256], BF16)
        score = S("score", [128, 256])
        swork = S("swork", [128, 256])
        m8 = S("m8", [128, 16])
        zc = S("zc", [128, 16])
        ps = [P.psum("ps%d" % i, [128, 512]) for i in range(8)]

        def mm(out_ap, lhsT, rhs, start, stop, reads, writes):
            P.op("pe", lambda e: e.matmul(out_ap, lhsT=lhsT, rhs=rhs, start=start, stop=stop, skip_group_check=True), reads=reads, writes=writes, wap=out_ap)

        def tr(out_ap, in_ap, idn, reads, writes):
            P.op("pe", lambda e: e.transpose(out=out_ap, in_=in_ap, identity=idn.ap[:, :]), reads=list(reads) + [idn], writes=writes, wap=out_ap)

        def act(out_ap, in_ap, func, reads, writes, **kw):
            P.op("act", lambda e: e.activation(out=out_ap, in_=in_ap, func=func, **kw), reads=reads, writes=writes, wap=(out_ap if 'accum_out' not in kw else None))

        def tt(out_ap, in0, in1, op, reads, writes, eng="dve"):
            P.op(eng, lambda e: e.tensor_tensor(out=out_ap, in0=in0, in1=in1, op=op), reads=reads, writes=writes, wap=out_ap)

        def ts(out_ap, in0, s1, s2, op0, op1, reads, writes):
            if op1 is None:
                P.op("dve", lambda e: e.tensor_scalar(out=out_ap, in0=in0, scalar1=s1, scalar2=None, op0=op0), reads=reads, writes=writes, wap=out_ap)
            else:
                P.op("dve", lambda e: e.tensor_scalar(out=out_ap, in0=in0, scalar1=s1, scalar2=s2, op0=op0, op1=op1), reads=reads, writes=writes, wap=out_ap)

        def stt(out_ap, in0, sc, in1, op0, op1, reads, writes):
            P.op("dve", lambda e: e.scalar_tensor_tensor(out=out_ap, in0=in0, scalar=sc, in1=in1, op0=op0, op1=op1), reads=reads, writes=writes, wap=out_ap)

        def cp(out_ap, in_ap, reads, writes, eng="dve"):
            if eng == "act":
                P.op("act", lambda e: e.copy(out=out_ap, in_=in_ap), reads=reads, writes=writes, wap=out_ap)
            else:
                P.op(eng, lambda e: e.tensor_copy(out=out_ap, in_=in_ap), reads=reads, writes=writes, wap=out_ap)

        gel_i = [0]

        def gelu(out_ap, src_ap, n, p, src_bufs, out_bufs):
            t = tA[gel_i[0] % 2]
            gel_i[0] += 1
            tv = t.ap[0:p, 0:n]
            act(tv, src_ap, AF.Square, src_bufs, [t])
            ts(tv, tv, 0.044715, 1.0, ALU.mult, ALU.add, [t], [t])
            tt(tv, tv, src_ap, ALU.mult, [t] + src_bufs, [t])
            act(tv, tv, AF.Sigmoid, [t], [t], scale=GELU_C)
            tt(out_ap, tv, src_ap, ALU.mult, [t] + src_bufs, out_bufs)

        wb_i = [0]

        def load_w(blk):
            b = wbuf[wb_i[0] % 2]
            wb_i[0] += 1
            wg = wgroup_of(blk)
            P.dma("sp", b.ap[:, :], WTb[LCUR[0]][wg].ap[blk - WG_START[wg], :, :], b, WTb[LCUR[0]][wg])
            return b

        w16 = lambda b: b.ap.rearrange("p (k c) -> p k c", c=512)
        w64 = lambda b: b.ap.rearrange("p (k c) -> p k c", c=128)
        gb_i = [0]

        def gbank():
            b = ps[gb_i[0] % 4]
            gb_i[0] += 1
            return b

        def rstd_from(ssq_t, col_in, col_out):
            act(ssq_t.ap[:, col_out:col_out + 1], ssq_t.ap[:, col_in:col_in + 1], AF.Sqrt, [ssq_t, eps_b], [ssq_t], scale=1.0 / D, bias=eps_b.ap[:, 0:1])
            P.op("dve", lambda e: e.reciprocal(out=ssq_t.ap[:, col_out:col_out + 1], in_=ssq_t.ap[:, col_out:col_out + 1]), reads=[ssq_t], writes=[ssq_t])
        def ld(dst, src_buf, src_ap=None, q="sp", dst_ap=None):
            P.dma(q, dst.ap if dst_ap is None else dst_ap, src_buf.ap if src_ap is None else src_ap, dst, src_buf)

        ld(ident, identd)
        ld(normgT, normgTd)
        ld(cmask, cmaskd)
        ld(cmpm, cmpmd)
        ld(wmask, wmaskd)
        ld(ovb, ovbd)
        ld(invc, invcd)
        ld(hsel, hseld)
        P.op("dve", lambda e: e.memset(eps_b.ap[:, :], EPS), writes=[eps_b])
        P.op("dve", lambda e: e.memset(eps5.ap[:, :], 1e-5), writes=[eps5])
        for a_ in apS:
            P.op("dve", lambda e, a_=a_: e.memset(a_.ap[:, :, :], 0.0), writes=[a_])
        for v_ in vstg:
            P.op("dve", lambda e, v_=v_: e.memset(v_.ap[:, :, 65:66], 0.0), writes=[v_], wap=v_.ap[:, :, 65:66])
            P.op("dve", lambda e, v_=v_: e.memset(v_.ap[:, :, 64:65], 1.0), writes=[v_], wap=v_.ap[:, :, 64:65])
        P.op("dve", lambda e: e.memset(vcmp.ap[:, :, :, 65:66], 0.0), writes=[vcmp], wap=vcmp.ap[:, :, :, 65:66])
        P.op("dve", lambda e: e.memset(vcmp.ap[:, :, :, 64:65], 1.0), writes=[vcmp], wap=vcmp.ap[:, :, :, 64:65])
        cp(identb.ap[:, :], ident.ap[:, :], [ident], [identb])

        for l in range(NL):
            for wg in (5, 0, 1, 2, 3, 4):
                for b_ in range(WG_SIZE[wg]):
                    for q_ in range(4):
                        P.dma("pool", WTb[l][wg].ap[b_, :, q_ * 2048:(q_ + 1) * 2048], wsrc.ap[l, WG_START[wg] + b_, :, q_ * 2048:(q_ + 1) * 2048], WTb[l][wg], wsrc)

        ca = S2T
        ld(S1T, cTd)
        act(ca.ap[:, :], S1T.ap[:, :], AF.Silu, [S1T], [ca])
        ld(ggrow[0], badad, badad.ap[0:1, :], dst_ap=ggrow[0].ap[0:1, 0:1536])
        ld(ggrow[1], badad, badad.ap[1:2, :], dst_ap=ggrow[1].ap[0:1, 0:1536])
        wi = 0
        for l in range(2):
            for n in range(3):
                bank = ps[wi % 2]
                wi += 1
                for kq in range(4):
                    wt = xt[kq % 2]
                    P.dma("sp", wt.ap[:, :].rearrange("p (k c) -> p k c", c=512), wadad.ap[l, kq * 512:(kq + 1) * 512, n * 512:(n + 1) * 512].rearrange("(k p) c -> p k c", p=128), wt, wadad)
                    for k4 in range(4):
                        kc = kq * 4 + k4
                        mm(bank.ap[0:1, :], ca.ap[:, kc:kc + 1], wt.ap[:, k4 * 512:(k4 + 1) * 512], kc == 0, kc == 15, [ca, wt], [bank])
                tt(ggrow[l].ap[0:1, n * 512:(n + 1) * 512], bank.ap[0:1, :], ggrow[l].ap[0:1, n * 512:(n + 1) * 512], ALU.add, [bank, ggrow[l]], [ggrow[l]])
        for l in range(2):
            P.dma("sp", modx.ap[l:l + 1, :], ggrow[l].ap[0:1, 0:1536], modx, ggrow[l], own=ggrow[l])
        P.cc_allgather(modx, modg)

        def mod_pieces(l, col0, n):
            c = col0
            while c < col0 + n:
                r = c // 1536
                e = min(col0 + n, (r + 1) * 1536)
                yield c - col0, e - c, modg.ap[2 * r + l:2 * r + l + 1, c - 1536 * r:e - 1536 * r]
                c = e

        def make_hT(xtile, sl, GT, ST):
            sq = ssq[sl]
            for b_ in range(4):
                act(junk.ap[:, :], xtile.ap[:, b_ * 512:(b_ + 1) * 512], AF.Square, [xtile], [junk, sq], accum_out=sq.ap[:, b_:b_ + 1])
            P.op("dve", lambda e, sq=sq: e.tensor_reduce(out=sq.ap[:, 6:7], in_=sq.ap[:, 0:4], axis=AX.X, op=ALU.add), reads=[sq], writes=[sq])
            rstd_from(sq, 6, 7)
            ts(xs.ap[:, :], xtile.ap[:, :], sq.ap[:, 7:8], None, ALU.mult, None, [xtile, sq], [xs])
            for b_ in range(4):
                bank = ps[4 + b_]
                for i in range(4):
                    kc = b_ * 4 + i
                    tr(bank.ap[:, i * 128:(i + 1) * 128], xs.ap[:, kc * 128:(kc + 1) * 128], ident, [xs], [bank])
                for i in range(4):
                    kc = b_ * 4 + i
                    act(hT.ap[:, kc, sl * 128:(sl + 1) * 128], bank.ap[:, i * 128:(i + 1) * 128], AF.Identity, [bank, GT, ST], [hT],
                        scale=GT.ap[:, kc:kc + 1], bias=ST.ap[:, kc:kc + 1])

        stg_i = [0]

        def groupA(G):
            l = LCUR[0]
            c0 = G * NT
            xsrc = x if l == 0 else xcur
            for sl in range(2):
                P.dma("sp", xt[sl].ap[:, :], xsrc.ap[c0 + sl * 128:c0 + (sl + 1) * 128, :], xt[sl], xsrc)
                make_hT(xt[sl], sl, G1T, S1T)
            P.dma("pool", hTs.ap[:, :, c0:c0 + NT].rearrange("k p n -> p k n"), hT.ap[:, :, :], hTs, hT, own=hT)
            wb = load_w(WA + 0)
            for m in range(4):
                bank = gbank()
                for kc in range(16):
                    mm(bank.ap[:, 0:NT], w16(wb)[:, kc, m * 128:(m + 1) * 128], hT.ap[:, kc, :], kc == 0, kc == 15, [wb, hT], [bank])
                t = tA[m % 2]
                cp(t.ap[:, 0:NT], bank.ap[:, 0:NT], [bank], [t], eng="act")
                P.dma("pool", apTs.ap[m, :, c0:c0 + NT], t.ap[:, 0:NT], apTs, t, own=t)
                for sl in range(2):
                    P.dma("pool", Hx[l].ap[m * 128:(m + 1) * 128, (2 * G + sl) * 16:(2 * G + sl + 1) * 16], t.ap[:, sl * 128 + 112:sl * 128 + 128], Hx[l], t, own=t)
            for wi_, (kA, kB, vkind) in enumerate(((0, 1, None), (2, None, 0), (3, None, 1))):
                wb = load_w(WA + 1 + wi_)
                tiles = [(kA, 0, 0), (kA, 1, 128)] + ([(kB, 0, 256), (kB, 1, 384)] if kB is not None else [])
                for kind, mp, col in tiles:
                    bank = gbank()
                    for kc in range(16):
                        mm(bank.ap[:, 0:NT], w16(wb)[:, kc, col:col + 128], hT.ap[:, kc, :], kc == 0, kc == 15, [wb, hT], [bank])
                    pt = PT[stg_i[0] % 3]
                    stg_i[0] += 1
                    cp(pt.ap[:, 0:NT], bank.ap[:, 0:NT], [bank], [pt])
                    P.dma("pool", KTx[l].ap[kind * 256 + mp * 128:kind * 256 + (mp + 1) * 128, c0:c0 + NT], pt.ap[:, 0:NT], KTx[l], pt, own=pt)
                if vkind is not None:
                    for sl in range(2):
                        bank = gbank()
                        for kc in range(16):
                            mm(bank.ap[:, 0:256], hT.ap[:, kc, sl * 128:(sl + 1) * 128], w16(wb)[:, kc, 256:512], kc == 0, kc == 15, [wb, hT], [bank])
                        vs_ = vstg[stg_i[0] % 2]
                        stg_i[0] += 1
                        cp(vs_.ap[:, :, 0:64], bank.ap[:, 0:256].rearrange("p (g c) -> p g c", c=64), [bank], [vs_], eng="act")
                        r0 = vkind * 2048 + c0 + sl * 128
                        P.dma("pool", Vx[l].ap[r0:r0 + 128, :], vs_.ap[:, :, :].rearrange("p g c -> p (g c)"), Vx[l], vs_, own=vs_)

        def prepB():
            l = LCUR[0]
            ld(psc, pscd, pscd.ap[l])
            ld(b1, b1d, b1d.ap[l])
            ld(b2k, b2kd, b2kd.ap[l], dst_ap=b2k.ap[0:64, :])
            ld(b2k, b2kd, b2kd.ap[l], dst_ap=b2k.ap[64:128, :])
            ld(lng, lngd, lngd.ap[l:l + 1, :].partition_broadcast(128))
            ld(lnb, lnbd, lnbd.ap[l:l + 1, :].partition_broadcast(128))
            ld(bsb, bsd, bsd.ap[l:l + 1, :].partition_broadcast(128))
            ld(b2v, b2vd, b2vd.ap[l:l + 1, :].partition_broadcast(128))
            ld(wng, wngd, wngd.ap[l], q="pool")
            ld(poolw, poolwd, poolwd.ap[l], q="pool")
            ld(posT, posTd, posTd.ap[l], q="pool")
            ld(w1, w1d, w1d.ap[l], q="pool")
            ld(w2, w2d, w2d.ap[l], q="pool")
            ld(w2k2, w2d, w2d.ap[l][:, 0, :], q="pool", dst_ap=w2k2.ap[:, 0:64])
            ld(w2k2, w2d, w2d.ap[l][:, 0, :], q="pool", dst_ap=w2k2.ap[:, 64:128])
            ld(xs, wsd, wsd.ap[l], dst_ap=xs.ap[:, 0:512].rearrange("p (g j) -> p g j", j=128))
            ld(tA[0], trild, dst_ap=tA[0].ap[:, 0:128])
            for g in range(4):
                tt(xs.ap[:, g * 128:(g + 1) * 128], xs.ap[:, g * 128:(g + 1) * 128], tA[0].ap[:, 0:128], ALU.mult, [xs, tA[0]], [xs])
            for g in range(4):
                tr(ps[4].ap[:, g * 128:(g + 1) * 128], xs.ap[:, g * 128:(g + 1) * 128], ident, [xs], [ps[4]])
            cp(WmT.ap[:, :, :], ps[4].ap[:, :].rearrange("p (g i) -> p g i", i=128), [ps[4]], [WmT])
            for kind in range(2):
                for p_ in range(32):
                    mm(ps[5].ap[:, kind:kind + 1], w1.ap[:, kind, p_, :], posT.ap[:, kind, p_:p_ + 1], p_ == 0 and kind == 0, p_ == 31, [w1, posT], [ps[5]])
            tt(c1.ap[:, :], ps[5].ap[:, 0:2], b1.ap[:, :], ALU.add, [ps[5], b1], [c1])
            kci = 0
            KG = KTg[l]
            for kind in range(2):
                for g in range(4):
                    for half in range(2):
                        kb = kcb[kci % 2]
                        kci += 1
                        if half == 1:
                            P.op("dve", lambda e, kb=kb: e.memset(kb.ap[:, 8192:8224], 0.0), writes=[kb], wap=kb.ap[:, 8192:8224])
                        for s8 in range(8):
                            slot = half * 8 + s8
                            P.dma("sp", kb.ap[:, s8 * 1024:(s8 + 1) * 1024].rearrange("d (r j) -> d r j", j=128),
                                  KG.ap[:, slot * 128:(slot + 1) * 128].rearrange("(r k g d) j -> k g d r j", r=8, k=4, g=4)[kind, g], kb, KG)
                        if half == 0:
                            P.dma("sp", kb.ap[:, 8192:8224], KG.ap[kind * 256 + g * 64:kind * 256 + (g + 1) * 64, 1024:1056], kb, KG)
                        kv3 = kb.ap[:, :].rearrange("d (n p) -> d n p", p=16)
                        bank = gbank()
                        for p_ in range(32):
                            rhs = kv3[:, (p_ // 16):(p_ // 16) + 512, p_ % 16]
                            mm(bank.ap[:, :], w1.ap[:, kind, p_, :], rhs, p_ == 0, p_ == 31, [w1, kb], [bank])
                        act(xs.ap[:, 0:512], bank.ap[:, :], AF.Identity, [bank, c1], [xs], bias=c1.ap[:, kind:kind + 1])
                        hid = PT[kci % 3]
                        gelu(hid.ap[:, :], xs.ap[:, 0:512], 512, 128, [xs], [hid])
                        if kind == 0:
                            b2 = gbank()
                            mm(b2.ap[:, :], w2k2.ap[:, :], hid.ap[:, :], True, True, [w2k2, hid], [b2])
                            act(kcmpT.ap[:, g, half * 512:(half + 1) * 512], b2.ap[:, :], AF.Identity, [b2, b2k], [kcmpT], bias=b2k.ap[:, 0:1])
                        else:
                            b2 = gbank()
                            for nt in range(4):
                                mm(b2.ap[:, nt * 64:(nt + 1) * 64], hid.ap[:, nt * 128:(nt + 1) * 128], w2.ap[:, 1, :], nt == 0, nt == 3, [w2, hid], [b2])
                            for nt in range(4):
                                tt(vcmp.ap[:, g, half * 4 + nt, 0:64], b2.ap[:, nt * 64:(nt + 1) * 64], b2v.ap[:, :], ALU.add, [b2, b2v], [vcmp])

        PDEPTH = 2
        job_i = [0]

        def attention(sl, s):
            l = LCUR[0]
            KG, VG = KTg[l], Vg[l]
            qv = lambda g: qT.ap[:, 4 * g:4 * g + 4, sl * 128:(sl + 1) * 128]
            gv = lambda g, b: gates.ap[:, sl, g * 12 + b:g * 12 + 12:3]
            SB = [ps[0], ps[1], ps[3]]
            pend = []

            def submit(job):
                for f in job.get("pre", []):
                    f()
                K = job["K"]
                g = job["g"]
                bank = SB[job_i[0] % 3]
                pt = PT[job_i[0] % 4]
                job_i[0] += 1
                mm(bank.ap[0:K, :], job["kT"], qv(g), True, True, job["kreads"] + [qT], [bank])
                act(pt.ap[0:K, :], bank.ap[0:K, :], AF.Exp, [bank], [pt])
                pv_ = pt.ap[0:K, :].rearrange("p (h q) -> p h q", q=128)
                for (m_ap, m_reads) in job.get("masks", []):
                    tt(pv_, pv_, m_ap.unsqueeze(1).to_broadcast([K, 4, 128]), ALU.mult, [pt] + m_reads, [pt])
                pend.append((job, pt))
                if len(pend) > PDEPTH:
                    flush_one()

            def flush_one():
                job, pt = pend.pop(0)
                K = job["K"]
                for (h, out_ap, rhs_ap, st_, sp_, rd, bank) in job["pv"]:
                    mm(out_ap, pt.ap[0:K, h * 128:(h + 1) * 128], rhs_ap, st_, sp_, [pt] + rd, [bank])

            def flush_all():
                while pend:
                    flush_one()

            def combine(accb, g, b):
                zv = accb.ap[:, 0:264].rearrange("p (h c) -> p h c", c=66)[:, :, 64]
                ts(zc.ap[:, 0:4], zv, 1e-30, None, ALU.max, None, [accb], [zc])
                P.op("dve", lambda e: e.reciprocal(out=zc.ap[:, 4:8], in_=zc.ap[:, 0:4]), reads=[zc], writes=[zc])
                tt(zc.ap[:, 8:12], zc.ap[:, 4:8], gv(g, b), ALU.mult, [zc, gates], [zc])
                for h in range(4):
                    o = yc.ap[:, (4 * g + h) * 64:(4 * g + h + 1) * 64]
                    a = accb.ap[:, h * 66:h * 66 + 64]
                    stt(o, a, zc.ap[:, 8 + h:9 + h], o, ALU.mult, ALU.add, [accb, zc, yc], [yc])

            def load_kv_ranks(kb, vb, kkind, vkind, r0, slot):
                for g in range(4):
                    for hf in range(2):
                        P.dma("sp", kb.ap[hf * 64:(hf + 1) * 64, g, :].rearrange("d (r j) -> d r j", j=128),
                              KG.ap[:, slot * 128:(slot + 1) * 128].rearrange("(r k g d) j -> k g d r j", r=8, k=4, g=4)[kkind, g][:, r0:r0 + 4, :], kb, KG)
                P.dma("sp", vb.ap[:, :, :, :].rearrange("p t g c -> p t (g c)"),
                      VG.ap[:, :].rearrange("(r k t p) c -> k t p r c", r=8, k=2, t=16, p=128)[vkind, slot][:, r0:r0 + 4, :], vb, VG)

            P.dma("sp", bon.ap[:, :], bonusd.ap[s, :, :], bon, bonusd)
            nf = s // 2
            for g in range(4):
                accC = ps[4:8]
                tiles = [(i * 128, 128, None) for i in range(nf)]
                if s % 2 == 1:
                    tiles.append((nf * 128, 128, 1))
                else:
                    tiles.append((nf * 128, 64, 0))
                for ti, (n0, K, mk) in enumerate(tiles):
                    tile_idx = n0 // 128
                    pv = []
                    for h in range(4):
                        pv.append((h, accC[h].ap[:, 0:65], vcmp.ap[0:K, g, tile_idx, 0:65], ti == 0, False, [vcmp], accC[h]))
                        pv.append((h, accC[h].ap[:, 128:384], ovb.ap[0:K, 224 - 32 * tile_idx:480 - 32 * tile_idx], False, ti == len(tiles) - 1, [ovb], accC[h]))
                    job = dict(K=K, g=g, kT=kcmpT.ap[:, g, n0:n0 + K], kreads=[kcmpT], pv=pv)
                    if mk is not None:
                        job["masks"] = [(cmpm.ap[0:K, mk, :], [cmpm])]
                    submit(job)
                flush_all()
                for h in range(4):
                    ts(zc.ap[:, h:h + 1], accC[h].ap[:, 64:65], 1e-30, None, ALU.max, None, [accC[h]], [zc])
                P.op("dve", lambda e: e.reciprocal(out=zc.ap[:, 4:8], in_=zc.ap[:, 0:4]), reads=[zc], writes=[zc])
                ts(score.ap[:, :], accC[0].ap[:, 128:384], zc.ap[:, 4:5], None, ALU.mult, None, [accC[0], zc], [score])
                for h in range(1, 4):
                    stt(score.ap[:, :], accC[h].ap[:, 128:384], zc.ap[:, 4 + h:5 + h], score.ap[:, :], ALU.mult, ALU.add, [accC[h], zc, score], [score])
                tt(score.ap[:, :], score.ap[:, :], bon.ap[:, :], ALU.add, [score, bon], [score])
                P.op("dve", lambda e: e.max(out=m8.ap[:, 0:8], in_=score.ap[:, :]), reads=[score], writes=[m8])
                P.op("dve", lambda e: e.match_replace(out=swork.ap[:, :], in_to_replace=m8.ap[:, 0:8], in_values=score.ap[:, :], imm_value=-1e30), reads=[score, m8], writes=[swork])
                P.op("dve", lambda e: e.max(out=m8.ap[:, 8:16], in_=swork.ap[:, :]), reads=[swork], writes=[m8])
                P.op("dve", lambda e, g=g: e.tensor_scalar(out=sel.ap[:, g, :].rearrange("p (r s t) -> p s r t", r=8, s=16, t=2),
                                                        in0=score.ap[:, :].rearrange("p (s r t) -> p s r t", s=16, r=8, t=2),
                                                        scalar1=m8.ap[:, 15:16], scalar2=None, op0=ALU.is_ge), reads=[score, m8], writes=[sel], wap=sel.ap[:, g, :])
                tt(zc.ap[:, 8:12], zc.ap[:, 4:8], gv(g, 0), ALU.mult, [zc, gates], [zc])
                for h in range(4):
                    ts(yc.ap[:, (4 * g + h) * 64:(4 * g + h + 1) * 64], accC[h].ap[:, 0:64], zc.ap[:, 8 + h:9 + h], None, ALU.mult, None, [accC[h], zc], [yc])

            chunks = []
            for r in range(8):
                s0 = 0
                while s0 < s:
                    n = min(4, s - s0)
                    chunks.append(("f", r, s0, n))
                    s0 += n
            chunks += [("d", 0, s, 4), ("d", 4, s, 4)]
            sx = selx[0]
            mbk = ps[2]
            mb = mbk.ap[:, 0:512].bitcast(BF16)
            njobs = sum(c[3] for c in chunks) * 4
            jn = 0
            for ci, (ck, r, s0, n) in enumerate(chunks):
                kb = kch[ci % 2]
                vb = vch[ci % 2]

                def chunk_pre(ck=ck, r=r, s0=s0, n=n, kb=kb, vb=vb):
                    if ck == "f":
                        for hf in range(2):
                            P.dma("sp", kb.ap[hf * 64:(hf + 1) * 64, :, 0:n * 128], KG.ap[r * 1024 + 512:r * 1024 + 768, s0 * 128:(s0 + n) * 128].rearrange("(g d) k -> d g k", g=4), kb, KG)
                        P.dma("sp", vb.ap[:, 0:n, :, :], VG.ap[r * 4096 + s0 * 128:r * 4096 + (s0 + n) * 128, :].rearrange("(t p) (g c) -> p t g c", p=128, g=4), vb, VG)
                        P.op("pool", lambda e: e.tensor_copy(out=sx.ap[:, :, 0:n * 128].rearrange("p g (b k) -> p g b k", k=64),
                                                            in_=sel.ap[:, :, r * 32 + 2 * s0:r * 32 + 2 * (s0 + n)].unsqueeze(3).to_broadcast([128, 4, 2 * n, 64])),
                             reads=[sel], writes=[sx])
                    else:
                        load_kv_ranks(kb, vb, 2, 0, r, s)
                        for g in range(4):
                            P.op("pool", lambda e, g=g: e.tensor_copy(out=sx.ap[:, g, :].rearrange("p (r t k) -> p r t k", r=4, t=2),
                                                                   in_=sel.ap[:, g, :].rearrange("p (r s t) -> p r s t", r=8, s=16)[:, r:r + 4, s, :].unsqueeze(3).to_broadcast([128, 4, 2, 64])),
                                 reads=[sel], writes=[sx])

                for h0 in range(0, n, 2):
                    nt2 = min(2, n - h0)

                    def half_pre(h0=h0, nt2=nt2):
                        for g in range(4):
                            for t2 in range(nt2):
                                t_ = h0 + t2
                                tr(mb[:, g * 256 + t2 * 128:g * 256 + (t2 + 1) * 128], sx.ap[:, g, t_ * 128:(t_ + 1) * 128], identb, [sx], [mbk])

                    first = True
                    for g in range(4):
                        accS = ps[4 + g]
                        for t2 in range(nt2):
                            t_ = h0 + t2
                            masks = [(mb[:, g * 256 + t2 * 128:g * 256 + (t2 + 1) * 128], [mbk])]
                            if ck == "d":
                                masks.append((cmask.ap[:, r + t_, :], [cmask]))
                            pv = [(h, accS.ap[:, h * 66:h * 66 + 65], vb.ap[:, t_, g, 0:65], ci == 0 and t_ == 0 and h == 0,
                                   ci == len(chunks) - 1 and t_ == n - 1 and h == 3, [vb], accS) for h in range(4)]
                            job = dict(K=128, g=g, kT=kb.ap[:, g, t_ * 128:(t_ + 1) * 128], kreads=[kb], masks=masks, pv=pv)
                            if first:
                                job["pre"] = ([chunk_pre] if h0 == 0 else []) + [half_pre]
                                first = False
                            submit(job)
            flush_all()
            for g in range(4):
                combine(ps[4 + g], g, 1)

            wch = ([(4, s - 1, 0)] if s > 0 else []) + [(0, s, 4), (4, s, 8)]
            for wi_, (r0, slot, m0) in enumerate(wch):
                kb = kch[wi_ % 2]
                vb = vch[wi_ % 2]
                first = True
                for g in range(4):
                    accW = ps[4 + g]
                    for t_ in range(4):
                        pv = [(h, accW.ap[:, h * 66:h * 66 + 65], vb.ap[:, t_, g, 0:65], wi_ == 0 and t_ == 0 and h == 0,
                               wi_ == len(wch) - 1 and t_ == 3 and h == 3, [vb], accW) for h in range(4)]
                        job = dict(K=128, g=g, kT=kb.ap[:, g, t_ * 128:(t_ + 1) * 128], kreads=[kb], masks=[(wmask.ap[:, m0 + t_, :], [wmask])], pv=pv)
                        if first:
                            job["pre"] = [lambda kb=kb, vb=vb, r0=r0, slot=slot: load_kv_ranks(kb, vb, 3, 1, r0, slot)]
                            first = False
                        submit(job)
            flush_all()
            for g in range(4):
                combine(ps[4 + g], g, 2)
            for b_ in range(2):
                for i in range(4):
                    m = b_ * 4 + i
                    tr(ps[b_].ap[:, i * 128:(i + 1) * 128], yc.ap[:, m * 128:(m + 1) * 128], ident, [yc], [ps[b_]])
                cp(ycT.ap[:, b_ * 4:b_ * 4 + 4, sl * 128:(sl + 1) * 128], ps[b_].ap[:, :].rearrange("p (m q) -> p m q", q=128), [ps[b_]], [ycT], eng="act")

        def norm_residual(sl, tok0, src_fm, gg, xin, xout_store):
            sq = ssq[sl]
            for b_ in range(4):
                bank = ps[4 + b_]
                for i in range(4):
                    m = b_ * 4 + i
                    tr(bank.ap[:, i * 128:(i + 1) * 128], src_fm.ap[:, m, sl * 128:(sl + 1) * 128], ident, [src_fm], [bank])
                act(junk.ap[:, 0:512], bank.ap[:, :], AF.Square, [bank], [junk, sq], accum_out=sq.ap[:, b_:b_ + 1])
            P.op("dve", lambda e: e.tensor_reduce(out=sq.ap[:, 4:5], in_=sq.ap[:, 0:4], axis=AX.X, op=ALU.add), reads=[sq], writes=[sq])
            rstd_from(sq, 4, 5)
            P.dma("sp", xs.ap[:, :], ggd.ap[gg:gg + 1, :].partition_broadcast(128), xs, ggd)
            for b_ in range(4):
                bank = ps[4 + b_]
                stt(xs.ap[:, b_ * 512:(b_ + 1) * 512], bank.ap[:, :], sq.ap[:, 5:6], xs.ap[:, b_ * 512:(b_ + 1) * 512], ALU.mult, ALU.mult, [bank, sq, xs], [xs])
            tt(xin.ap[:, :], xin.ap[:, :], xs.ap[:, :], ALU.add, [xin, xs], [xin])
            if xout_store:
                xdst = xo if LCUR[0] == NL - 1 else xcur
                P.dma("pool", xdst.ap[tok0:tok0 + 128, :], xin.ap[:, :], xdst, xin, own=xin)
        def groupB(G):
            c0 = G * NT
            P.dma("sp", hT.ap[:, :, :], hTs.ap[:, :, c0:c0 + NT].rearrange("k p n -> p k n"), hT, hTs)
            for sl in range(2):
                xsrc = x if LCUR[0] == 0 else xcur
                P.dma("sp", xt[sl].ap[:, :], xsrc.ap[c0 + sl * 128:c0 + (sl + 1) * 128, :], xt[sl], xsrc)
            wb = load_w(WU)
            for m in range(4):
                bank = gbank()
                for kc in range(16):
                    mm(bank.ap[:, 0:NT], w16(wb)[:, kc, m * 128:(m + 1) * 128], hT.ap[:, kc, :], kc == 0, kc == 15, [wb, hT], [bank])
                gelu(uT.ap[:, m, :], bank.ap[:, 0:NT], NT, 128, [bank], [uT])
            wb = load_w(WV)
            for sl in range(2):
                bank = gbank()
                for kc in range(16):
                    mm(bank.ap[:, :], hT.ap[:, kc, sl * 128:(sl + 1) * 128], w16(wb)[:, kc, :], kc == 0, kc == 15, [wb, hT], [bank])
                gelu(xs.ap[:, 0:512], bank.ap[:, :], 512, 128, [bank], [xs])
                P.op("dve", lambda e: e.bn_stats(out=bnst.ap[:, 0:6], in_=xs.ap[:, 0:512]), reads=[xs], writes=[bnst])
                P.op("dve", lambda e: e.bn_aggr(out=bnst.ap[:, 6:8], in_=bnst.ap[:, 0:6]), reads=[bnst], writes=[bnst])
                act(bnst.ap[:, 7:8], bnst.ap[:, 7:8], AF.Sqrt, [bnst, eps5], [bnst], scale=1.0, bias=eps5.ap[:, 0:1])
                P.op("dve", lambda e: e.reciprocal(out=bnst.ap[:, 7:8], in_=bnst.ap[:, 7:8]), reads=[bnst], writes=[bnst])
                ts(xs.ap[:, 0:512], xs.ap[:, 0:512], bnst.ap[:, 6:7], bnst.ap[:, 7:8], ALU.subtract, ALU.mult, [xs, bnst], [xs])
                tt(xs.ap[:, 0:512], xs.ap[:, 0:512], lng.ap[:, :], ALU.mult, [xs, lng], [xs])
                tt(vN.ap[:, :], xs.ap[:, 0:512], lnb.ap[:, :], ALU.add, [xs, lnb], [vN])
                bank = gbank()
                for g in range(4):
                    mm(bank.ap[:, g * 128:(g + 1) * 128], vN.ap[:, g * 128:(g + 1) * 128], WmT.ap[:, g, :], g == 0, g == 3, [vN, WmT], [bank])
                tt(xs.ap[:, 0:512], bank.ap[:, :], bsb.ap[:, :], ALU.add, [bank, bsb], [xs])
                tt(ybT.ap[:, :, sl * 128:(sl + 1) * 128], xs.ap[:, 0:512].rearrange("p (g i) -> p g i", i=128), uT.ap[:, :, sl * 128:(sl + 1) * 128], ALU.mult, [xs, uT], [ybT])
            Hg_l = Hg[LCUR[0]]
            for g in range(4):
                P.dma("sp", apb.ap[:, g, :, 16:144], apTs.ap[g, :, c0:c0 + NT].rearrange("c (s t) -> c s t", t=128), apb, apTs)
            for sl in range(2):
                s = 2 * G + sl
                for g in range(4):
                    P.dma("sp", cand.ap[:, g, 0:8, :], Hg_l.ap[:, s * 16:(s + 1) * 16].rearrange("(r g c) t -> g c r t", r=8, g=4)[g], cand, Hg_l)
                    if s > 0:
                        P.dma("sp", cand.ap[:, g, 8, :], Hg_l.ap[7 * 512 + g * 128:7 * 512 + (g + 1) * 128, (s - 1) * 16:s * 16], cand, Hg_l)
                nopt = 9 if s > 0 else 8
                hv = apb.ap[:, :, sl, 0:16]
                for o in range(nopt):
                    if o == 0:
                        ts(hv, cand.ap[:, :, 0, :], hsel.ap[:, 0:1], None, ALU.mult, None, [cand, hsel], [apb])
                    else:
                        stt(hv, cand.ap[:, :, o, :], hsel.ap[:, o:o + 1], hv, ALU.mult, ALU.add, [cand, hsel, apb], [apb])
            for g in range(4):
                src = apb.ap[:, g, :, :]
                cur = src
                rd = [apb]
                sh = 1
                for it in range(g + 1):
                    dst = apS[it % 2]
                    tt(dst.ap[:, :, sh:144], cur[:, :, sh:144], cur[:, :, 0:144 - sh], ALU.add, rd, [dst])
                    cur = dst.ap[:, :, :]
                    rd = [dst]
                    sh *= 2
                for sl in range(2):
                    s = 2 * G + sl
                    if s == 0:
                        tt(xs.ap[:, sl * 128:(sl + 1) * 128], cur[:, sl, 16:144], invc.ap[:, g, :], ALU.mult, rd + [invc], [xs])
                    else:
                        ts(xs.ap[:, sl * 128:(sl + 1) * 128], cur[:, sl, 16:144], 1.0 / (2 << g), None, ALU.mult, None, rd, [xs])
                    tt(pooled.ap[:, g, sl * 128:(sl + 1) * 128], xs.ap[:, sl * 128:(sl + 1) * 128], apb.ap[:, g, sl, 16:144], ALU.subtract, [xs, apb], [pooled_b])
                bank = gbank()
                mm(bank.ap[:, 0:NT], poolw.ap[:, g, :], pooled.ap[:, g, :], True, True, [poolw, pooled_b], [bank])
                act(yaT.ap[:, g, :], bank.ap[:, 0:NT], AF.Identity, [bank, psc], [yaT], scale=psc.ap[:, g:g + 1])
            q4 = qT.ap[:, :, :].rearrange("p (g h) n -> p g h n", h=4)
            P.op("dve", lambda e: e.memset(q4[64:128, :, 0:2, :], 0.0), writes=[qT], wap=q4[64:128, :, 0:2, :])
            P.op("dve", lambda e: e.memset(q4[0:64, :, 2:4, :], 0.0), writes=[qT], wap=q4[0:64, :, 2:4, :])
            for qb_ in range(2):
                wb = load_w(WQ + qb_)
                for mi in range(4):
                    g_ = 2 * qb_ + mi // 2
                    p_ = mi % 2
                    bank = gbank()
                    for kc in range(16):
                        mm(bank.ap[:, 0:NT], w16(wb)[:, kc, mi * 128:(mi + 1) * 128], hT.ap[:, kc, :], kc == 0, kc == 15, [wb, hT], [bank])
                    act(qT.ap[0:64, 4 * g_ + p_, :], bank.ap[0:64, 0:NT], AF.Identity, [bank], [qT], scale=0.125)
                    act(qT.ap[64:128, 4 * g_ + 2 + p_, :], bank.ap[64:128, 0:NT], AF.Identity, [bank], [qT], scale=0.125)
            for sl in range(2):
                bank = gbank()
                for kc in range(16):
                    mm(bank.ap[:, 0:48], hT.ap[:, kc, sl * 128:(sl + 1) * 128], wng.ap[:, kc, :], kc == 0, kc == 15, [wng, hT], [bank])
                act(gates.ap[:, sl, :], bank.ap[:, 0:48], AF.Sigmoid, [bank], [gates])
            for sl in range(2):
                if do_attn:
                    attention(sl, 2 * G + sl)
                else:
                    P.op("dve", lambda e, sl=sl: e.memset(ycT.ap[:, :, sl * 128:(sl + 1) * 128], 0.0), writes=[ycT])
            for m in range(16):
                wb = load_w(WM + m)
                wv = w64(wb)
                for b_ in range(3):
                    bank = gbank()
                    for kc in range(16):
                        mm(bank.ap[:, 0:NT], wv[:, b_ * 16 + kc, :], hT.ap[:, kc, :], kc == 0, kc == 15, [wb, hT], [bank])
                    act(gsb[b_].ap[:, :], bank.ap[:, 0:NT], AF.Sigmoid, [bank], [yc])
                for b_, (src, nk, k0) in enumerate(((yaT, 4, 48), (ybT, 4, 52), (ycT, 8, 56))):
                    bank = gbank()
                    for kc in range(nk):
                        mm(bank.ap[:, 0:NT], wv[:, k0 + kc, :], src.ap[:, kc, :], kc == 0, kc == nk - 1, [wb, src], [bank])
                    if b_ == 0:
                        tt(macc.ap[:, 0:NT], bank.ap[:, 0:NT], gsb[0].ap[:, :], ALU.mult, [bank, yc], [macc])
                    else:
                        tt(mtmp.ap[:, 0:NT], bank.ap[:, 0:NT], gsb[b_].ap[:, :], ALU.mult, [bank, yc], [mtmp])
                        if b_ == 1:
                            tt(macc.ap[:, 0:NT], macc.ap[:, 0:NT], mtmp.ap[:, 0:NT], ALU.add, [macc, mtmp], [macc])
                        else:
                            tt(mergedT.ap[:, m, :], macc.ap[:, 0:NT], mtmp.ap[:, 0:NT], ALU.add, [macc, mtmp], [mergedT])
            for blk in range(4):
                wb = load_w(WO + blk)
                for mi in range(4):
                    bank = gbank()
                    for kc in range(16):
                        mm(bank.ap[:, 0:NT], w16(wb)[:, kc, mi * 128:(mi + 1) * 128], mergedT.ap[:, kc, :], kc == 0, kc == 15, [wb, mergedT], [bank])
                    cp(yfT.ap[:, blk * 4 + mi, :], bank.ap[:, 0:NT], [bank], [yfT], eng="act")
            for sl in range(2):
                norm_residual(sl, c0 + sl * 128, yfT, 0, xt[sl], False)
            for sl in range(2):
                sq = ssq[sl]
                for b_ in range(4):
                    act(junk.ap[:, :], xt[sl].ap[:, b_ * 512:(b_ + 1) * 512], AF.Square, [xt[sl]], [junk, sq], accum_out=sq.ap[:, b_:b_ + 1])
                P.op("dve", lambda e, sq=sq: e.tensor_reduce(out=sq.ap[:, 6:7], in_=sq.ap[:, 0:4], axis=AX.X, op=ALU.add), reads=[sq], writes=[sq])
                rstd_from(sq, 6, 7)
                ts(xs.ap[:, :], xt[sl].ap[:, :], sq.ap[:, 7:8], None, ALU.mult, None, [xt[sl], sq], [xs])
                for b_ in range(4):
                    bank = ps[4 + b_]
                    for i in range(4):
                        kc = b_ * 4 + i
                        tr(bank.ap[:, i * 128:(i + 1) * 128], xs.ap[:, kc * 128:(kc + 1) * 128], ident, [xs], [bank])
                    for i in range(4):
                        kc = b_ * 4 + i
                        act(hT.ap[:, kc, sl * 128:(sl + 1) * 128], bank.ap[:, i * 128:(i + 1) * 128], AF.Identity, [bank, G2T, S2T], [hT],
                            scale=G2T.ap[:, kc:kc + 1], bias=S2T.ap[:, kc:kc + 1])
            for blk in range(16):
                wb = load_w(WF1 + blk)
                for mi in range(4):
                    bank = gbank()
                    for kc in range(16):
                        mm(bank.ap[:, 0:NT], w16(wb)[:, kc, mi * 128:(mi + 1) * 128], hT.ap[:, kc, :], kc == 0, kc == 15, [wb, hT], [bank])
                    t = tA[(blk * 4 + mi) % 2]
                    act(t.ap[:, 0:NT], bank.ap[:, 0:NT], AF.Relu, [bank], [t])
                    tt(f1T.ap[:, blk * 4 + mi, :], t.ap[:, 0:NT], t.ap[:, 0:NT], ALU.mult, [t], [f1T])
            for m in range(16):
                wb = load_w(WF2 + m)
                bank = gbank()
                for kc in range(64):
                    mm(bank.ap[:, 0:NT], w64(wb)[:, kc, :], f1T.ap[:, kc, :], kc == 0, kc == 63, [wb, f1T], [bank])
                cp(yfT.ap[:, m, :], bank.ap[:, 0:NT], [bank], [yfT], eng="act")
            for sl in range(2):
                norm_residual(sl, c0 + sl * 128, yfT, 1, xt[sl], True)


        do_attn = True
        for l in range(NL):
            LCUR[0] = l
            for i in range(6):
                for kc in range(16):
                    for off, ln, sap in mod_pieces(l, i * 2048 + kc * 128, 128):
                        P.dma("sp", modT.ap[:, i, kc:kc + 1], sap.rearrange("a (p b) -> (a p) b", b=1), modT, modg)
            stt(G1T.ap[:, :], modT.ap[:, 1, :], 1.0, normgT.ap[:, l, 0, :], ALU.add, ALU.mult, [modT, normgT], [G1T])
            cp(S1T.ap[:, :], modT.ap[:, 0, :], [modT], [S1T])
            stt(G2T.ap[:, :], modT.ap[:, 4, :], 1.0, normgT.ap[:, l, 2, :], ALU.add, ALU.mult, [modT, normgT], [G2T])
            cp(S2T.ap[:, :], modT.ap[:, 3, :], [modT], [S2T])
            for gi_, (mi, gi) in enumerate(((2, 1), (5, 3))):
                for off, ln, sap in mod_pieces(l, mi * 2048, 2048):
                    P.dma("sp", ggrow[0].ap[0:1, off:off + ln], sap, ggrow[0], modg)
                ld(ggrow[1], normgd, normgd.ap[l, gi:gi + 1, :])
                tt(ggrow[0].ap[:, :], ggrow[0].ap[:, :], ggrow[1].ap[:, :], ALU.mult, ggrow, [ggrow[0]])
                P.dma("sp", ggd.ap[gi_:gi_ + 1, :], ggrow[0].ap[:, :], ggd, ggrow[0], own=ggrow[0])
            for G in range(ngrp):
                groupA(G)
            P.cc_allgather(KTx[l], KTg[l])
            P.cc_allgather(Vx[l], Vg[l])
            P.cc_allgather(Hx[l], Hg[l])
            prepB()
            for G in range(ngrpB if ngrpB is not None else ngrp):
                groupB(G)
        P.finish([xo])
        print("fused instructions:", P.n_inst)
    return nc

import ml_dtypes
BF = ml_dtypes.bfloat16
POOL_WINDOWS = (2, 4, 8, 16)
TP = 16384 + 128


def core_rows(a, c):
    return np.ascontiguousarray(a.reshape(16, 8, 128, *a.shape[1:])[:, c].reshape(2048, *a.shape[1:]))


def uncore_rows(parts):
    a = np.stack(parts, 0)
    a = a.reshape(8, 16, 128, *a.shape[2:])
    a = np.moveaxis(a, 0, 1)
    return np.ascontiguousarray(a.reshape(16384, *a.shape[3:]))


def uncore_last(parts):
    a = np.stack(parts, -2)
    a = a.reshape(*a.shape[:-1], 16, 128)
    a = np.moveaxis(a, -3, -2)
    return np.ascontiguousarray(a.reshape(*a.shape[:-3], 16384))


def tile16(W):
    return W.reshape(16, 128, 512).transpose(1, 0, 2).reshape(128, 8192)


def tile64(W):
    return W.reshape(64, 128, 128).transpose(1, 0, 2).reshape(128, 8192)


def weight_blocks(inp, l):
    w_in = inp['w_in'][l]
    blocks = []
    blocks.append(tile16(w_in[:, 512:1024]))
    blocks.append(tile16(w_in[:, 1024:1536]))
    wq = w_in[:, 1536:2560].reshape(2048, 4, 4, 64)[:, :, [0, 2, 1, 3], :].reshape(2048, 1024)
    blocks.append(tile16(wq[:, 0:512]))
    blocks.append(tile16(wq[:, 512:1024]))
    bg = w_in[:, 4144:10288]
    for m in range(16):
        cs = slice(m * 128, (m + 1) * 128)
        rows = np.concatenate([bg[:, 0 * 2048 + m * 128:0 * 2048 + (m + 1) * 128], bg[:, 2048 + m * 128:2048 + (m + 1) * 128], bg[:, 4096 + m * 128:4096 + (m + 1) * 128],
                               inp['w_br_pool'][l][:, cs], inp['w_br_gmlp'][l][:, cs], inp['w_br_nsa'][l][:, cs]], axis=0)
        blocks.append(tile64(rows))
    for b in range(4):
        blocks.append(tile16(inp['w_out'][l][:, b * 512:(b + 1) * 512]))
    for b in range(16):
        blocks.append(tile16(inp['w_ff1'][l][:, b * 512:(b + 1) * 512]))
    for m in range(16):
        blocks.append(tile64(inp['w_ff2'][l][:, m * 128:(m + 1) * 128]))
    return np.ascontiguousarray(np.stack(blocks, 0), dtype=np.float32)


def core_consts(c):
    k = np.arange(128)[:, None]
    q = np.arange(128)[None, :]
    cmask = np.zeros((128, 8, 128), np.float32)
    for i in range(8):
        if i < c:
            cmask[:, i, :] = 1
        elif i == c:
            cmask[:, i, :] = (k <= q)
    nl = np.arange(64)[:, None]
    cm = (16 * nl + 31 <= 128 * c + q).astype(np.float32)
    cmpm = np.zeros((128, 2, 128), np.float32)
    cmpm[0:64, 0] = cm
    cmpm[0:64, 1] = 1
    cmpm[64:128, 1] = cm
    wm = np.zeros((128, 2, 5, 128), np.float32)
    for r in range(5):
        base = (k > q) if r == 0 else ((k <= q) if r == 4 else np.ones((128, 128)))
        wm[:, 1, r] = base
        wm[:, 0, r] = base * (1.0 if c - 4 + r >= 0 else 0.0)
    bonus = np.zeros((16, 128, 256), np.float32)
    for s in range(16):
        qb = 8 * s + c
        cur = 2 * qb + (np.arange(128) >= 64)
        bonus[s, :, 0] = 1000
        bonus[s, np.arange(128), cur] = 1000
        ok = cur - 1 >= 0
        bonus[s, np.arange(128)[ok], (cur - 1)[ok]] = 1000
    ovb = np.zeros((128, 480), np.float32)
    for n in range(128):
        ovb[n, 224 + n // 4] = 1
        if n % 4 == 3:
            ovb[n, 225 + n // 4] = 1
    invc = np.zeros((128, 4, 128), np.float32)
    t = np.arange(128)
    for g, w in enumerate(POOL_WINDOWS):
        invc[:, g, :] = (1.0 / np.minimum(t + 1, w)) if c == 0 else 1.0 / w
    return dict(cmask=cmask.astype(BF), cmpm=cmpm.astype(BF), wmask=wm.astype(BF), bonus=bonus.astype(BF), ovb=ovb.astype(BF), invc=invc,
                ident=np.eye(128, dtype=np.float32), tril=np.tril(np.ones((128, 128), np.float32)))


def layer_small_inputs(inp, l):
    d = {}
    d['wng'] = np.ascontiguousarray(inp['w_in'][l][:, 4096:4144].reshape(16, 128, 48).transpose(1, 0, 2))
    d['poolw'] = np.ascontiguousarray(inp['pool_w'][l].transpose(1, 0, 2))
    d['pscT'] = np.ascontiguousarray(inp['pool_scale'][l].reshape(4, 128).T)
    d['lng'] = inp['gmlp_ln_g'][l].reshape(1, 512)
    d['lnb'] = inp['gmlp_ln_b'][l].reshape(1, 512)
    d['ws'] = np.ascontiguousarray(inp['gmlp_ws'][l].transpose(1, 0, 2))
    d['bs'] = inp['gmlp_bs'][l].reshape(1, 512)
    d['posT'] = np.ascontiguousarray(inp['cmp_pos'][l].transpose(2, 0, 1))
    d['w1t'] = np.ascontiguousarray(inp['cmp_w1'][l].reshape(2, 32, 64, 128).transpose(2, 0, 1, 3))
    d['b1T'] = np.ascontiguousarray(inp['cmp_b1'][l].T)
    d['w2'] = np.ascontiguousarray(inp['cmp_w2'][l].transpose(1, 0, 2))
    d['b2kT'] = np.ascontiguousarray(inp['cmp_b2'][l][0].reshape(64, 1))
    d['b2v'] = np.ascontiguousarray(inp['cmp_b2'][l][1].reshape(1, 64))
    return d


def modT_of(mod_l):
    return np.ascontiguousarray(mod_l.reshape(6, 16, 128).transpose(2, 0, 1))


def normgT_of(ng):
    return np.ascontiguousarray(ng.reshape(4, 16, 128).transpose(2, 0, 1))


def make_B_inputs(inp, l, mod_l, x_cores, resA, consts):
    KT = uncore_last([np.asarray(r['KT_o']) for r in resA])
    Vg = uncore_rows([np.asarray(r['V_o']).transpose(1, 0, 2) for r in resA])
    apT = uncore_last([np.asarray(r['apT_o']) for r in resA])
    WTb = np.concatenate([np.asarray(r['wdst']) for r in resA], 0)
    KTall = np.zeros((3, 4, 64, TP), BF)
    KTall[:, :, :, :16384] = KT[[0, 1, 2]]
    Vall = np.zeros((16384, 4, 66), BF)
    Vall[:, :, :64] = Vg[:, 0].reshape(16384, 4, 64)
    Vall[:, :, 64] = 1
    kwpad = np.zeros((4, 64, 512 + 16384), BF)
    kwpad[:, :, 512:] = KT[3]
    vwpad = np.zeros((512 + 16384, 4, 66), BF)
    vwpad[512:, :, :64] = Vg[:, 1].reshape(16384, 4, 64)
    vwpad[512:, :, 64] = 1
    appad = np.zeros((4, 128, 16 + 16384), np.float32)
    appad[:, :, 16:] = apT
    small = layer_small_inputs(inp, l)
    maps = []
    for c in range(8):
        kw = np.zeros((4, 64, 16, 640), BF)
        vw = np.zeros((16, 640, 4, 66), BF)
        halo = np.zeros((4, 128, 16, 16), np.float32)
        for s in range(16):
            qb = 8 * s + c
            kw[:, :, s, :] = kwpad[:, :, 128 * qb:128 * qb + 640]
            vw[s] = vwpad[128 * qb:128 * qb + 640]
            halo[:, :, s, :] = appad[:, :, 128 * qb:128 * qb + 16]
        m = dict(x=x_cores[c], hT=np.asarray(resA[c]['hT_o']), apT=np.asarray(resA[c]['apT_o']), halo=halo, modT=modT_of(mod_l), modrow=np.ascontiguousarray(mod_l),
                 normg=np.ascontiguousarray(inp['norm_g'][l]), normgT=normgT_of(inp['norm_g'][l]), KTall=KTall, Vall=Vall, kwTw=kw, vww=vw, WT=WTb)
        m.update(small)
        m.update({k: v for k, v in consts[c].items()})
        maps.append(m)
    return maps


def make_A_inputs(inp, l, mod_l, x_cores, wblocks):
    w_in = inp['w_in'][l]
    wA = np.ascontiguousarray(np.concatenate([w_in[:, 0:512], w_in[:, 2560:4096]], axis=1))
    maps = []
    for c in range(8):
        maps.append(dict(x=x_cores[c], modT=modT_of(mod_l), normgT=normgT_of(inp['norm_g'][l]), wA=wA, ident=np.eye(128, dtype=np.float32),
                         wsrc=np.ascontiguousarray(wblocks[7 * c:7 * c + 7])))
    return maps


def weight_blocks_F(inp, l):
    w_in = inp['w_in'][l]
    wA = np.concatenate([w_in[:, 0:512], w_in[:, 2560:4096]], axis=1)
    b56 = weight_blocks(inp, l)
    extra = np.stack([tile16(wA[:, j * 512:(j + 1) * 512]) for j in range(4)], 0).astype(np.float32)
    return np.ascontiguousarray(np.concatenate([b56, extra], 0))


def core_consts_F(c):
    d = core_consts(c)
    k = np.arange(128)[:, None]
    q = np.arange(128)[None, :]
    wm12 = np.zeros((128, 12, 128), np.float32)
    for i in range(12):
        r = i - c
        if 0 <= r <= 4:
            wm12[:, i] = (k > q) if r == 0 else ((k <= q) if r == 4 else 1.0)
    hsel = np.zeros((128, 9), np.float32)
    hsel[:, (c - 1) if c >= 1 else 8] = 1.0
    d.pop('wmask')
    d['wmask12'] = wm12.astype(BF)
    d['hsel'] = hsel
    return d


def make_F_inputs(inp):
    x = np.ascontiguousarray(inp['x'][0], dtype=np.float32)
    wsrc = np.stack([weight_blocks_F(inp, l) for l in range(2)], 0)
    smalls = [layer_small_inputs(inp, l) for l in range(2)]
    st = {k: np.ascontiguousarray(np.stack([smalls[0][k], smalls[1][k]], 0)) for k in smalls[0]}
    for k in ('lng', 'lnb', 'bs', 'b2v'):
        st[k] = np.ascontiguousarray(st[k].reshape(2, -1))
    cT = np.ascontiguousarray(inp['c'][0].reshape(16, 128).T)
    normg = np.ascontiguousarray(inp['norm_g'], dtype=np.float32)
    normgT = np.ascontiguousarray(normg.reshape(2, 4, 16, 128).transpose(3, 0, 1, 2))
    maps = []
    for c in range(8):
        m = dict(x=core_rows(x, c), cT=cT, wada=np.ascontiguousarray(inp['w_ada'][:, :, 1536 * c:1536 * (c + 1)]), bada=np.ascontiguousarray(inp['b_ada'][:, 1536 * c:1536 * (c + 1)]),
                 normg=normg, normgT=normgT, wsrc=wsrc)
        m.update(st)
        m.update(core_consts_F(c))
        maps.append(m)
    return maps

from concourse.bass_utils import run_bass_kernel_spmd

_PROGS = {}


def kernel(**inputs):
    inp = {k: np.asarray(v) for k, v in inputs.items()}
    cores = list(range(8))
    if "F" not in _PROGS:
        _PROGS["F"] = build_F()
    maps = make_F_inputs(inp)
    res = run_bass_kernel_spmd(_PROGS["F"], maps, core_ids=cores).results
    out = uncore_rows([np.asarray(r['xo'], dtype=np.float32) for r in res])
    return np.ascontiguousarray(out.reshape(1, 16384, 2048), dtype=np.float32)
```
